# Optimizing a Trainium2 kernel written in Bass

```python
import math
import jax, jax.numpy as jnp
from jax import lax
import numpy as np

D_MODEL = 2048
BATCH = 4
SEQ = 4096
DEPTH = 2

HEAD_DIM = 128
N_HEADS = D_MODEL // HEAD_DIM
N_HEADS_FOX = N_HEADS // 2
N_HEADS_DIL = N_HEADS - N_HEADS_FOX
D_FOX = N_HEADS_FOX * HEAD_DIM
D_DIL = N_HEADS_DIL * HEAD_DIM
D_MIX = D_FOX + D_DIL
SPLIT_SIZES = (D_FOX, D_FOX, D_FOX, N_HEADS_FOX, D_DIL, D_DIL, D_DIL)
SPLIT_POINTS = tuple(int(s) for s in np.cumsum(SPLIT_SIZES)[:-1])
N_IN = int(sum(SPLIT_SIZES))
D_FF = 4 * D_MODEL
Q_BLOCK = 128
DIL_PATTERNS = ((128, 1), (512, 4), (2048, 16))
DIL_BLOCK = 128
REL_BUCKETS = 32
REL_MAX_DISTANCE = 2048
NORM_EPS = 1e-6
NEG_INF = -1e30

kernel_name = "hymba_fox_dilated_hybrid"


def rms_norm(x, g):
    xf = x.astype(jnp.float32)
    y = xf * lax.rsqrt(jnp.mean(xf * xf, axis=-1, keepdims=True) + NORM_EPS)
    return (y * g.astype(jnp.float32)).astype(x.dtype)


def fox_attention(q, k, v, log_f):
    B, S, H, E = q.shape
    scale = E ** -0.5
    c = jnp.cumsum(log_f, axis=1).transpose(0, 2, 1)
    qh = q.transpose(0, 2, 1, 3)
    kh = k.transpose(0, 2, 1, 3)
    vh = v.transpose(0, 2, 1, 3)
    k_pos = jnp.arange(S)
    n_blk = S // Q_BLOCK

    def block(i):
        start = i * Q_BLOCK
        qb = lax.dynamic_slice_in_dim(qh, start, Q_BLOCK, axis=2)
        cb = lax.dynamic_slice_in_dim(c, start, Q_BLOCK, axis=2)
        logits = jnp.einsum('bhqe,bhke->bhqk', qb, kh,
                            preferred_element_type=jnp.float32) * scale
        logits = logits + (cb[..., :, None] - c[..., None, :])
        q_pos = start + jnp.arange(Q_BLOCK)
        causal = k_pos[None, :] <= q_pos[:, None]
        logits = jnp.where(causal, logits, NEG_INF)
        p = jax.nn.softmax(logits, axis=-1)
        return jnp.einsum('bhqk,bhke->bqhe', p.astype(v.dtype), vh)

    out = lax.map(block, jnp.arange(n_blk))
    return out.transpose(1, 0, 2, 3, 4).reshape(B, S, H, E)


def rel_bucket(dist):
    max_exact = REL_BUCKETS // 2
    d = jnp.maximum(dist.astype(jnp.float32), 1.0)
    large = max_exact + (jnp.log(d / max_exact) / math.log(REL_MAX_DISTANCE / max_exact)
                         * (REL_BUCKETS - max_exact)).astype(jnp.int32)
    large = jnp.minimum(large, REL_BUCKETS - 1)
    return jnp.where(dist < max_exact, dist, large)


def dilated_pattern(q, k, v, rel_bias, window, dilation):
    B, S, H, E = q.shape
    scale = E ** -0.5
    span = window // dilation
    group = dilation * DIL_BLOCK
    s_pad = -(-S // group) * group
    pad = s_pad - S
    n_sub = s_pad // dilation
    n_blk = n_sub // DIL_BLOCK

    def to_blocks(t):
        t = jnp.pad(t, ((0, 0), (0, pad), (0, 0), (0, 0)))
        t = t.reshape(B, n_sub, dilation, H, E).transpose(0, 2, 3, 1, 4)
        return t.reshape(B, dilation, H, n_blk, DIL_BLOCK, E)

    def with_prev(t):
        prev = jnp.pad(t, ((0, 0), (0, 0), (0, 0), (1, 0), (0, 0), (0, 0)))[:, :, :, :-1]
        return jnp.concatenate([prev, t], axis=4)

    qb = to_blocks(q)
    kc = with_prev(to_blocks(k))
    vc = with_prev(to_blocks(v))
    logits = jnp.einsum('bdhnqe,bdhnke->bdhnqk', qb, kc,
                        preferred_element_type=jnp.float32) * scale
    i = jnp.arange(DIL_BLOCK)[:, None]
    j = jnp.arange(2 * DIL_BLOCK)[None, :]
    rel = DIL_BLOCK + i - j
    in_band = (rel >= 0) & (rel <= span)
    bias = rel_bias.astype(jnp.float32)[rel_bucket(jnp.clip(rel, 0, span) * dilation)]
    bias = bias.transpose(2, 0, 1)
    key_valid = (jnp.arange(n_blk)[:, None, None] > 0) | (j[None] >= DIL_BLOCK)
    mask = in_band[None] & key_valid
    logits = logits + bias[None, None, :, None]
    logits = jnp.where(mask[None, None, None], logits, NEG_INF)
    m = jnp.max(logits, axis=-1, keepdims=True)
    p = jnp.exp(logits - m)
    s = jnp.sum(p, axis=-1, keepdims=True)
    o = jnp.einsum('bdhnqk,bdhnke->bdhnqe', p, vc.astype(jnp.float32)) / s
    lse = (m + jnp.log(s))[..., 0]

    def from_blocks(t):
        tail = t.shape[5:]
        t = t.reshape((B, dilation, H, n_sub) + tail)
        t = jnp.moveaxis(t, 3, 1)
        return t.reshape((B, s_pad, H) + tail)[:, :S]

    return from_blocks(o), from_blocks(lse)


def dilated_attention(q, k, v, rel_bias):
    outs, lses = [], []
    for window, dilation in DIL_PATTERNS:
        o, l = dilated_pattern(q, k, v, rel_bias, window, dilation)
        outs.append(o)
        lses.append(l)
    alpha = jax.nn.softmax(jnp.stack(lses, axis=0), axis=0)
    return jnp.einsum('pbsh,pbshe->bshe', alpha, jnp.stack(outs, axis=0))


def setup_inputs(seed: int = 0) -> dict:
    key = jax.random.key(seed)
    ks = jax.random.split(key, 12)
    f32 = jnp.float32
    return {
        "x": jax.random.normal(ks[0], (BATCH, SEQ, D_MODEL), f32),
        "norm1_g": 1.0 + 0.02 * jax.random.normal(ks[1], (DEPTH, D_MODEL), f32),
        "w_in": jax.random.normal(ks[2], (DEPTH, D_MODEL, N_IN), f32) * D_MODEL ** -0.5,
        "forget_b": jax.random.uniform(ks[3], (DEPTH, N_HEADS_FOX), f32, 1.0, 4.0),
        "rel_bias": 0.5 * jax.random.normal(ks[4], (REL_BUCKETS, N_HEADS_DIL), f32),
        "outnorm_a_g": 1.0 + 0.02 * jax.random.normal(ks[5], (DEPTH, D_FOX), f32),
        "outnorm_b_g": 1.0 + 0.02 * jax.random.normal(ks[6], (DEPTH, D_DIL), f32),
        "w_out": jax.random.normal(ks[7], (DEPTH, D_MIX, D_MODEL), f32) * D_MIX ** -0.5,
        "norm2_g": 1.0 + 0.02 * jax.random.normal(ks[8], (DEPTH, D_MODEL), f32),
        "w_mlp_in": jax.random.normal(ks[9], (DEPTH, D_MODEL, D_FF), f32) * D_MODEL ** -0.5,
        "w_mlp_out": jax.random.normal(ks[10], (DEPTH, D_FF, D_MODEL), f32) * D_FF ** -0.5,
        "final_norm_g": 1.0 + 0.02 * jax.random.normal(ks[11], (D_MODEL,), f32),
    }


def reference(x, norm1_g, w_in, forget_b, rel_bias, outnorm_a_g, outnorm_b_g, w_out,
              norm2_g, w_mlp_in, w_mlp_out, final_norm_g):
    B, S, _ = x.shape
    for l in range(DEPTH):
        h = rms_norm(x, norm1_g[l])
        proj = h @ w_in[l]
        q_a, k_a, v_a, f_a, q_b, k_b, v_b = jnp.split(proj, SPLIT_POINTS, axis=-1)
        heads_a = lambda t: t.reshape(B, S, N_HEADS_FOX, HEAD_DIM)
        heads_b = lambda t: t.reshape(B, S, N_HEADS_DIL, HEAD_DIM)
        log_f = jax.nn.log_sigmoid(f_a.astype(jnp.float32) + forget_b[l].astype(jnp.float32))
        y_a = fox_attention(heads_a(q_a), heads_a(k_a), heads_a(v_a), log_f)
        y_a = y_a.reshape(B, S, D_FOX)
        y_b = dilated_attention(heads_b(q_b), heads_b(k_b), heads_b(v_b), rel_bias)
        y_b = y_b.reshape(B, S, D_DIL).astype(x.dtype)
        mixed = jnp.concatenate([rms_norm(y_a, outnorm_a_g[l]),
                                 rms_norm(y_b, outnorm_b_g[l])], axis=-1)
        x = x + mixed @ w_out[l]
        h = rms_norm(x, norm2_g[l])
        x = x + jnp.square(jax.nn.relu(h @ w_mlp_in[l])) @ w_mlp_out[l]
    return rms_norm(x, final_norm_g)
```

```python
import numpy as np
import ml_dtypes
from contextlib import ExitStack
import concourse.bass as bass
import concourse.mybir as mybir
from concourse.bass_utils import run_bass_kernel_spmd

F32 = mybir.dt.float32
BF16 = mybir.dt.bfloat16
AF = mybir.ActivationFunctionType
ALU = mybir.AluOpType

D = 2048
T = 2048
SEQ = 4096
NH = 8
E = 128
NIN = 6152
DFF = 8192
DEPTH = 2
EPS = 1e-6
SCALE = E ** -0.5
NEG = -30000.0
DIL = (1, 4, 16)
NCORES = 8


_UID = [0]


def un(name):
    _UID[0] += 1
    return f"{name}_{_UID[0]}"


class Res:
    __slots__ = ("name", "w", "r", "dsem", "dcnt")

    def __init__(self, name):
        self.name = name
        self.w = {}
        self.r = {}
        self.dsem = None
        self.dcnt = 0


class Sync:
    def __init__(self, nc, stack):
        self.nc = nc
        self.stack = stack
        self.eng = {"pe": nc.tensor, "act": nc.scalar, "dve": nc.vector, "pool": nc.gpsimd, "sp": nc.sync}
        self.sem = {}
        self.cnt = {}
        self.known = {}
        self.pending = {}
        for e in self.eng:
            self.sem[e] = stack.enter_context(nc.semaphore("s_" + e))
            self.cnt[e] = 0
            self.known[e] = {}
            self.pending[e] = False
        self.nsem = 5
        self.all_res = []

    def res(self, name):
        r = Res(un(name))
        self.all_res.append(r)
        return r

    def _wait(self, e, evs):
        kn = self.known[e]
        for (sem, val) in evs:
            key = sem.num
            if kn.get(key, 0) < val:
                self.eng[e].wait_ge(sem, val)
                kn[key] = val

    def _deps(self, reads, writes, parts=()):
        evs = []
        for r in reads:
            evs.extend(r.w.values())
        for w in writes:
            evs.extend(w.w.values())
            evs.extend(w.r.values())
        for w in parts:
            evs.extend(w.r.values())
        return evs

    @staticmethod
    def _put(d, ev):
        k = ev[0].num
        if k not in d or d[k][1] < ev[1]:
            d[k] = ev

    def _record(self, ev, reads, writes, parts=()):
        for r in reads:
            self._put(r.r, ev)
        for w in writes:
            w.w = {ev[0].num: ev}
            w.r = {}
        for w in parts:
            self._put(w.w, ev)

    def op(self, e, fn, reads=(), writes=(), inc=True):
        self._wait(e, self._deps(reads, writes))
        inst = fn(self.eng[e])
        ev = (self.sem[e], self.cnt[e] + 1)
        if inc:
            self.cnt[e] += 1
            inst.then_inc(self.sem[e], 1)
            self.pending[e] = False
        else:
            self.pending[e] = True
        self._record(ev, reads, writes)
        return inst

    def dsem_of(self, r):
        if r.dsem is None:
            r.dsem = self.stack.enter_context(self.nc.semaphore("d_" + r.name))
            self.nsem += 1
        return r.dsem

    def dma(self, q, pairs, reads=(), writes=(), parts=(), semres=None, **kw):
        self._wait(q, self._deps(reads, writes, parts))
        sr = semres if semres is not None else (writes[0] if writes else (parts[0] if parts else reads[0]))
        sem = self.dsem_of(sr)
        for (o, i) in pairs:
            self.eng[q].dma_start(out=o, in_=i, **kw).then_inc(sem, 16)
            sr.dcnt += 16
        ev = (sem, sr.dcnt)
        self._record(ev, reads, writes, parts)
        return ev

    def barrier(self):
        evs = []
        for e in self.eng:
            assert not self.pending[e], e
            if self.cnt[e]:
                evs.append((self.sem[e], self.cnt[e]))
        for r in self.all_res:
            if r.dsem is not None and r.dcnt:
                evs.append((r.dsem, r.dcnt))
        for e in self.eng:
            self._wait(e, evs)
        for r in self.all_res:
            r.w = {}
            r.r = {}


class Pool_:
    def __init__(self, S, stack, name, shape, dtype, n, psum=False):
        self.bufs = []
        for i in range(n):
            if psum:
                t = stack.enter_context(S.nc.psum_tensor(un(f"{name}{i}"), shape, dtype))
            else:
                t = stack.enter_context(S.nc.sbuf_tensor(un(f"{name}{i}"), shape, dtype))
            self.bufs.append((t, S.res(f"{name}{i}")))
        self.i = 0

    def next(self):
        b = self.bufs[self.i % len(self.bufs)]
        self.i += 1
        return b


CH = 1024 * 1024


def kv_ap(tensors, off, pattern):
    chunk, loc = off // CH, off % CH
    extent = sum((c - 1) * s for s, c in pattern)
    assert loc + extent < CH, (off, pattern)
    return dram_ap(tensors[chunk], loc, pattern)


def dram_ap(t, offset, pattern):
    return bass.AP(tensor=t, offset=offset, ap=[list(p) for p in pattern])


class Ctx:
    pass


def load_wtile(S, wpool, wdram_ap2d, k0, c0, ncols=512):
    wt, wr = wpool.next()
    src = wdram_ap2d[k0:k0 + 2048, c0:c0 + ncols].rearrange("(kc p) c -> p kc c", p=128)
    S.dma("pool", [(wt[:, :, 0:ncols], src)], writes=[wr])
    return wt, wr


def rmsnorm_to_hT(S, C, xt, xr, gbc, gr, hT, hTr, st, out_norm=None):
    junk, junkr = C.junk.next()
    ss, ssr = C.stat.next()
    S.op("act", lambda e: e.activation(out=junk[:], in_=xt[:], func=AF.Square, accum_out=ss[:, 0:1]),
         reads=[xr], writes=[junkr, ssr])
    S.op("act", lambda e: e.activation(out=ss[:, 1:2], in_=ss[:, 0:1], func=AF.Ln, scale=1.0 / D, bias=C.eps_t[:, 0:1]),
         reads=[ssr, C.constr], writes=[ssr])
    S.op("act", lambda e: e.activation(out=ss[:, 2:3], in_=ss[:, 1:2], func=AF.Exp, scale=-0.5),
         reads=[ssr], writes=[ssr])
    if out_norm is not None:
        ot, otr = out_norm
        S.op("dve", lambda e: e.scalar_tensor_tensor(out=ot[:], in0=xt[:], scalar=ss[:, 2:3], in1=gbc[:],
                                                     op0=ALU.mult, op1=ALU.mult),
             reads=[xr, ssr, gr], writes=[otr])
        return
    hb, hbr = C.hb.next()
    S.op("dve", lambda e: e.scalar_tensor_tensor(out=hb[:], in0=xt[:], scalar=ss[:, 2:3], in1=gbc[:],
                                                 op0=ALU.mult, op1=ALU.mult),
         reads=[xr, ssr, gr], writes=[hbr])
    transpose_rows(S, C, hb, hbr, hT, hTr, st)


def transpose_rows(S, C, hb, hbr, hT, hTr, st):
    for g in range(4):
        pt, ptr = C.pst.next()
        ptb = pt[:].bitcast(BF16)
        for j in range(4):
            kc = g * 4 + j
            S.op("pe", lambda e, kc=kc, j=j: e.transpose(out=ptb[:, j * 128:(j + 1) * 128],
                                                         in_=hb[:, kc * 128:(kc + 1) * 128], identity=C.ident[:]),
                 reads=[hbr, C.constr], writes=([ptr] if j == 0 else []), inc=(j == 3))
        src = ptb[:, 0:512].rearrange("p (j t) -> p j t", j=4)
        dst = hT[:, g * 4:(g + 1) * 4, st * 128:(st + 1) * 128]
        eng = "act" if (g % 2 == 0) else "dve"
        if eng == "act":
            S.op("act", lambda e: e.copy(out=dst, in_=src), reads=[ptr], writes=[])
        else:
            S.op("dve", lambda e: e.tensor_copy(out=dst, in_=src), reads=[ptr], writes=[])
        ev = (S.sem[eng], S.cnt[eng])
        S._put(hTr.w, ev)


def claim(S, e, res):
    S._wait(e, list(res.w.values()) + list(res.r.values()))


def build(debug_phase=None):
    nc = bass.Bass("TRN2", target_bir_lowering=False)
    dt = nc.dram_tensor
    x_in = dt("x", [T, D], F32, kind="ExternalInput")
    norm1_g = dt("norm1_g", [DEPTH, D], F32, kind="ExternalInput")
    w_in = dt("w_in", [DEPTH, D, NIN], F32, kind="ExternalInput")
    forget_b = dt("forget_b", [DEPTH, NH], F32, kind="ExternalInput")
    rel_bias = dt("rel_bias", [32, NH], F32, kind="ExternalInput")
    full = debug_phase in (None, "c")
    if full:
        on_a = dt("outnorm_a_g", [DEPTH, 1024], F32, kind="ExternalInput")
        on_b = dt("outnorm_b_g", [DEPTH, 1024], F32, kind="ExternalInput")
        w_out = dt("w_out", [DEPTH, D, D], F32, kind="ExternalInput")
        norm2_g = dt("norm2_g", [DEPTH, D], F32, kind="ExternalInput")
        w1 = dt("w_mlp_in", [DEPTH, D, DFF], F32, kind="ExternalInput")
        w2 = dt("w_mlp_out", [DEPTH, DFF, D], F32, kind="ExternalInput")
        fin_g = dt("final_norm_g", [D], F32, kind="ExternalInput")
    else:
        on_a = on_b = w_out = norm2_g = w1 = w2 = fin_g = None
    c_ident = dt("c_ident", [128, 128], BF16, kind="ExternalInput")
    c_tri = dt("c_tri", [128, 128], BF16, kind="ExternalInput")
    c_identf = dt("c_identf", [128, 128], F32, kind="ExternalInput")
    c_sel = dt("c_sel", [8, 8 * 128], F32, kind="ExternalInput")
    c_oh = dt("c_oh", [33, 3 * 384], F32, kind="ExternalInput")
    c_anti = dt("c_anti", [128, 128], F32, kind="ExternalInput")
    c_core = dt("c_core", [128, 2], F32, kind="ExternalInput")
    y_out = dt("y", [T, D], F32, kind="ExternalOutput")
    PAY = 4 * NH * E * T
    kv_loc = [[dt(f"kv_loc{l}_{i}", [1024, 1024], BF16) for i in range(8)] for l in range(DEPTH)]
    kv_all = [[dt(f"kv_all{l}_{i}", [2048, 1024], BF16) for i in range(8)] for l in range(DEPTH)]
    c_loc = [dt(f"c_loc{l}", [NH, T], F32) for l in range(DEPTH)]
    c_all = [dt(f"c_all{l}", [2 * NH, T], F32) for l in range(DEPTH)]
    qaT = dt("qaT", [NH, E, T], BF16)
    qbT = dt("qbT", [NH, E, T], BF16)
    yT = dt("yT", [2 * NH, E, T], BF16)
    x1 = dt("x1", [T, D], F32)
    x2 = dt("x2", [T, D], F32)
    x3 = dt("x3", [T, D], F32)
    gtab = dt("gtab", [NH, 3 * 384], F32)
    gtab2 = dt("gtab2", [NH, 128, 768], F32)

    dbg = {}
    if debug_phase is not None:
        dbg["qaT"] = dt("dbg_qaT", [NH, E, T], BF16, kind="ExternalOutput")
        dbg["kv"] = dt("dbg_kv", [PAY], BF16, kind="ExternalOutput")
        dbg["c"] = dt("dbg_c", [NH * T], F32, kind="ExternalOutput")
        dbg["qbT"] = dt("dbg_qbT", [NH, E, T], BF16, kind="ExternalOutput")
        dbg["yT"] = dt("dbg_yT", [2 * NH, E, T], BF16, kind="ExternalOutput")
        dbg["x2"] = dt("dbg_x2", [T, D], F32, kind="ExternalOutput")

    with ExitStack() as top:
        S = Sync(nc, top)
        C = Ctx()
        C.nc = nc
        sb = lambda name, shape, dtype: top.enter_context(nc.sbuf_tensor(un(name), shape, dtype))
        C.ident = sb("ident", [128, 128], BF16)
        C.tri = sb("tri", [128, 128], BF16)
        C.identf = sb("identf", [128, 128], F32)
        C.ones = sb("ones", [128, 128], BF16)
        C.eps_t = sb("eps_t", [128, 1], F32)
        C.core = sb("corec", [128, 2], F32)
        C.constr = S.res("const")
        S.dma("sp", [(C.ident[:], c_ident.ap()), (C.tri[:], c_tri.ap()), (C.identf[:], c_identf.ap()),
                     (C.core[:], c_core.ap())], writes=[C.constr])
        S.op("dve", lambda e: e.memset(C.ones[:], 1.0), writes=[])
        S.op("dve", lambda e: e.memset(C.eps_t[:], EPS), writes=[])
        S._put(C.constr.w, (S.sem["dve"], S.cnt["dve"]))

        ccsem = top.enter_context(nc.semaphore("ccsem"))
        cccnt = [0]
        if debug_phase != "a":
            phase_setup(nc, S, C, rel_bias, c_oh, c_anti, gtab, gtab2)
        for l in range(DEPTH):
            xsrc = x_in if l == 0 else x2
            xdst = x2 if l == 0 else x3
            phase_a(nc, S, C, l, xsrc, norm1_g, w_in, forget_b, qaT, qbT, kv_loc[l], c_loc[l])
            if debug_phase == "a":
                break
            if debug_phase == "s":
                break
            phase_b(nc, S, C, l, qaT, qbT, kv_loc[l], kv_all[l], c_loc[l], c_all[l], c_sel, gtab2, yT, ccsem, cccnt)
            if debug_phase == "b":
                break
            phase_c(nc, S, C, l, xsrc, xdst, x1, yT, on_a, on_b, w_out, norm2_g, w1, w2)
            if debug_phase == "c":
                break
        if debug_phase in ("b", "c", "s"):
            with ExitStack() as st:
                tb = st.enter_context(nc.sbuf_tensor("dbgbufy", [128, 16 * 2048], BF16))
                tr = S.res("dbgbufy")
                S.dma("sp", [(tb[:].rearrange("e (h t) -> e h t", h=16), yT.ap().rearrange("h e t -> e h t"))], writes=[tr])
                S.dma("sp", [(dbg["yT"].ap().rearrange("h e t -> e h t"), tb[:].rearrange("e (h t) -> e h t", h=16))], reads=[tr])
                tx = st.enter_context(nc.sbuf_tensor("dbgbufx", [128, 16 * 2048], F32))
                txr = S.res("dbgbufx")
                S.dma("sp", [(tx[:].rearrange("p (n d) -> p n d", n=16), x2.ap().rearrange("(n p) d -> p n d", p=128))], writes=[txr])
                S.dma("sp", [(dbg["x2"].ap().rearrange("(n p) d -> p n d", p=128), tx[:].rearrange("p (n d) -> p n d", n=16))], reads=[txr])
                S.barrier()
        if debug_phase is None:
            phase_final(nc, S, C, x3, fin_g, y_out)
        S.barrier()
    return nc


def phase_a(nc, S, C, l, xsrc, norm1_g, w_in, forget_b, qaT, qbT, kv_loc, c_loc):
    with ExitStack() as ph:
        sb = lambda name, shape, dtype: ph.enter_context(nc.sbuf_tensor(un(name), shape, dtype))
        C.junk = Pool_(S, ph, "junk", [128, D], BF16, 1)
        C.stat = Pool_(S, ph, "stat", [128, 4], F32, 4)
        C.hb = Pool_(S, ph, "hb", [128, D], BF16, 2)
        C.pst = Pool_(S, ph, "pst", [128, 512], F32, 2, psum=True)
        xpool = Pool_(S, ph, "xt", [128, D], F32, 3)
        wpool = Pool_(S, ph, "wt", [128, 16, 512], BF16, 3)
        hTp = Pool_(S, ph, "hT", [128, 16, 512], BF16, 2)
        psm = Pool_(S, ph, "psm", [128, 512], F32, 4, psum=True)
        psf = Pool_(S, ph, "psf", [128, 512], F32, 1, psum=True)
        stg = Pool_(S, ph, "stg", [128, 512], BF16, 4)
        gbc = sb("gbc", [128, D], F32)
        gr = S.res("gbc")
        wf = sb("wf", [128, 16, 8], BF16)
        wfr = S.res("wf")
        zf = sb("zf", [8, T], F32)
        zfr = S.res("zf")
        nb = sb("nb", [8, 2], F32)
        nbr = S.res("nb")
        zeros = sb("zeros", [8, T], F32)
        zr = S.res("zeros")

        S.dma("sp", [(gbc[:], dram_ap(norm1_g, l * D, [[0, 128], [1, D]]))], writes=[gr])
        wl = w_in.ap()[l]
        S.dma("pool", [(wf[:], wl[:, 3072:3080].rearrange("(kc p) c -> p kc c", p=128))], writes=[wfr])
        S.dma("sp", [(nb[:, 0:1], dram_ap(forget_b, l * NH, [[1, NH], [1, 1]]))], writes=[nbr])
        S.op("dve", lambda e: e.tensor_scalar(out=nb[:, 1:2], in0=nb[:, 0:1], scalar1=-1.0, scalar2=None, op0=ALU.mult),
             reads=[nbr], writes=[nbr])
        S.op("dve", lambda e: e.memset(zeros[:], 0.0), writes=[zr])

        kaT_o = 0
        kbT_o = NH * E * T
        va_o = 2 * NH * E * T
        vb_o = 3 * NH * E * T
        kvr = S.res("kvloc")
        qr = S.res("qdram")

        tiles = []
        for j in range(2):
            tiles.append((0 + 512 * j, "fm", ("qa", 4 * j)))
        for j in range(2):
            tiles.append((1024 + 512 * j, "fm", ("ka", 4 * j)))
        for j in range(2):
            tiles.append((2048 + 512 * j, "tm", ("va", 512 * j)))
        for j in range(2):
            tiles.append((3080 + 512 * j, "fm", ("qb", 4 * j)))
        for j in range(2):
            tiles.append((4104 + 512 * j, "fm", ("kb", 4 * j)))
        for j in range(2):
            tiles.append((5128 + 512 * j, "tm", ("vb", 512 * j)))

        for rt in range(4):
            t0 = rt * 512
            hT, hTr = hTp.next()
            claim(S, "act", hTr)
            claim(S, "dve", hTr)
            hTr.w = {}
            hTr.r = {}
            for st in range(4):
                xt, xr = xpool.next()
                S.dma("sp", [(xt[:], xsrc.ap()[t0 + st * 128:t0 + (st + 1) * 128, :])], writes=[xr])
                rmsnorm_to_hT(S, C, xt, xr, gbc, gr, hT, hTr, st)
            pf, pfr = psf.next()
            for kc in range(16):
                S.op("pe", lambda e, kc=kc: e.matmul(pf[0:8, :], lhsT=wf[:, kc, :], rhs=hT[:, kc, :],
                                                     start=(kc == 0), stop=(kc == 15)),
                     reads=[wfr, hTr], writes=([pfr] if kc == 0 else []), inc=(kc == 15))
            S.op("dve", lambda e: e.tensor_copy(out=zf[:, t0:t0 + 512], in_=pf[0:8, :]), reads=[pfr], writes=[])
            S._put(zfr.w, (S.sem["dve"], S.cnt["dve"]))
            for (c0, kind, dst) in tiles:
                wt, wr = load_wtile(S, wpool, wl, 0, c0)
                if kind == "fm":
                    name, h0 = dst
                    for hb_ in range(4):
                        pm, pmr = psm.next()
                        for kc in range(16):
                            S.op("pe", lambda e, kc=kc: e.matmul(pm[:], lhsT=wt[:, kc, hb_ * 128:(hb_ + 1) * 128],
                                                                 rhs=hT[:, kc, :], start=(kc == 0), stop=(kc == 15)),
                                 reads=[wr, hTr], writes=([pmr] if kc == 0 else []), inc=(kc == 15))
                        sg, sgr = stg.next()
                        S.op("act", lambda e: e.copy(out=sg[:], in_=pm[:]), reads=[pmr], writes=[sgr])
                        h = h0 + hb_
                        if name == "qa":
                            S.dma("sp", [(qaT.ap()[h, :, t0:t0 + 512], sg[:])], reads=[sgr], parts=[qr])
                        elif name == "qb":
                            S.dma("sp", [(qbT.ap()[h, :, t0:t0 + 512], sg[:])], reads=[sgr], parts=[qr])
                        else:
                            base = kaT_o if name == "ka" else kbT_o
                            dstap = kv_ap(kv_loc, base + h * E * T + t0, [[T, 128], [1, 512]])
                            S.dma("sp", [(dstap, sg[:])], reads=[sgr], parts=[kvr])
                else:
                    name, cc0 = dst
                    base = va_o if name == "va" else vb_o
                    for st in range(4):
                        pm, pmr = psm.next()
                        for kc in range(16):
                            S.op("pe", lambda e, kc=kc: e.matmul(pm[:], lhsT=hT[:, kc, st * 128:(st + 1) * 128],
                                                                 rhs=wt[:, kc, :], start=(kc == 0), stop=(kc == 15)),
                                 reads=[wr, hTr], writes=([pmr] if kc == 0 else []), inc=(kc == 15))
                        sg, sgr = stg.next()
                        S.op("dve", lambda e: e.tensor_copy(out=sg[:], in_=pm[:]), reads=[pmr], writes=[sgr])
                        dstap = kv_ap(kv_loc, base + (t0 + st * 128) * 1024 + cc0, [[1024, 128], [1, 512]])
                        S.dma("sp", [(dstap, sg[:])], reads=[sgr], parts=[kvr])
        ef = sb("ef", [8, T], F32)
        efr = S.res("ef")
        S.op("act", lambda e: e.activation(out=ef[:], in_=zf[:], func=AF.Exp, scale=-1.0, bias=nb[:, 1:2]),
             reads=[zfr, nbr], writes=[efr])
        S.op("act", lambda e: e.activation(out=ef[:], in_=ef[:], func=AF.Ln, scale=1.0, bias=1.0),
             reads=[efr], writes=[efr])
        S.op("dve", lambda e: e.tensor_tensor_scan(out=zf[:], data0=ef[:], data1=zeros[:], initial=0.0,
                                                   op0=ALU.add, op1=ALU.add),
             reads=[efr, zr], writes=[zfr])
        S.dma("sp", [(c_loc.ap(), zf[:])], reads=[zfr], parts=[kvr])
        S.barrier()


def phase_setup(nc, S, C, rel_bias, c_oh, c_anti, gtab, gtab2):
    with ExitStack() as ph:
        sb = lambda name, shape, dtype: ph.enter_context(nc.sbuf_tensor(un(name), shape, dtype))
        rb = sb("rbext", [33, NH], F32)
        rbr = S.res("rbext")
        oh = sb("oh", [33, 3 * 384], F32)
        ohr = S.res("oh")
        gt = sb("gt", [NH, 3 * 384], F32)
        gtr = S.res("gt")
        pp = Pool_(S, ph, "psg", [128, 512], F32, 3, psum=True)
        S.op("dve", lambda e: e.memset(rb[32:33, :], NEG), writes=[rbr])
        S.dma("sp", [(rb[0:32, :], rel_bias.ap())], parts=[rbr])
        S.dma("sp", [(oh[:], c_oh.ap())], writes=[ohr])
        for p in range(3):
            pg, pgr = pp.next()
            S.op("pe", lambda e: e.matmul(pg[0:NH, 0:384], lhsT=rb[:, :], rhs=oh[:, p * 384:(p + 1) * 384],
                                          start=True, stop=True), reads=[rbr, ohr], writes=[pgr])
            S.op("act", lambda e: e.activation(out=gt[:, p * 384:(p + 1) * 384], in_=pg[0:NH, 0:384], func=AF.Exp),
                 reads=[pgr], writes=[])
            S._put(gtr.w, (S.sem["act"], S.cnt["act"]))
        gdr = S.res("gtabd")
        S.dma("sp", [(gtab.ap(), gt[:])], reads=[gtr], parts=[gdr])
        anti = sb("anti", [128, 128], F32)
        antir = S.res("anti")
        S.dma("sp", [(anti[:], c_anti.ap())], writes=[antir])
        Xp = Pool_(S, ph, "X", [128, 768], F32, 2)
        Mp = Pool_(S, ph, "M", [128, 768], F32, 2)
        g2r = S.res("gtab2d")
        for h in range(NH):
            X, Xr = Xp.next()
            prs = []
            for p in range(3):
                for kt in range(2):
                    prs.append((X[:, (p * 2 + kt) * 128:(p * 2 + kt + 1) * 128],
                                dram_ap(gtab, h * 1152 + p * 384 + 129 - 128 * kt, [[1, 128], [1, 128]])))
            S.dma("sp", prs, reads=[gdr], writes=[Xr])
            M, Mr = Mp.next()
            for half in range(2):
                pg, pgr = pp.next()
                S.op("pe", lambda e: e.matmul(pg[:, 0:384], lhsT=anti[:], rhs=X[:, half * 384:(half + 1) * 384],
                                              start=True, stop=True), reads=[antir, Xr], writes=[pgr])
                S.op("dve", lambda e: e.tensor_copy(out=M[:, half * 384:(half + 1) * 384], in_=pg[:, 0:384]),
                     reads=[pgr], writes=([Mr] if half == 0 else []))
            S._put(Mr.w, (S.sem["dve"], S.cnt["dve"]))
            S.dma("sp", [(gtab2.ap()[h], M[:])], reads=[Mr], parts=[g2r])
        S.barrier()


def phase_b(nc, S, C, l, qaT, qbT, kv_loc, kv_all, c_loc, c_all, c_sel, gtab2, yT, ccsem, cccnt):
    with ExitStack() as ph:
        biasB = ph.enter_context(nc.sbuf_tensor(un("biasB"), [128, NH, 32, 16], F32))
        bBr = S.res("biasB")
        pairs = [[0, 1], [2, 3], [4, 5], [6, 7]]
        for i in range(8):
            nc.gpsimd.collective_compute("AllGather", ALU.bypass, replica_groups=pairs,
                                         ins=[kv_loc[i].ap().opt()], outs=[kv_all[i].ap().opt()]).then_inc(ccsem)
        nc.gpsimd.collective_compute("AllGather", ALU.bypass, replica_groups=pairs,
                                     ins=[c_loc.ap().opt()], outs=[c_all.ap().opt()]).then_inc(ccsem)
        cccnt[0] += 9
        for e in S.eng:
            S._wait(e, [(ccsem, cccnt[0])])
        with ExitStack() as tb:
            sb = lambda name, shape, dtype: tb.enter_context(nc.sbuf_tensor(un(name), shape, dtype))
            ncl = sb("ncl", [NH, T], F32)
            nclr = S.res("ncl")
            ncp = sb("ncp", [NH, T], F32)
            ncpr = S.res("ncp")
            sel = sb("sel", [NH, NH * 128], F32)
            selr = S.res("sel")
            cT = sb("cT", [128, 32, NH], F32)
            cTr = S.res("cT")
            ncref = sb("ncref", [128, NH, 16], F32)
            ncrr = S.res("ncref")
            S.dma("sp", [(ncl[:], c_loc.ap())], writes=[nclr])
            S.dma("sp", [(ncp[:], c_all.ap()[0:NH, :])], writes=[ncpr])
            S.dma("sp", [(sel[:], c_sel.ap())], writes=[selr])
            S.op("dve", lambda e: e.tensor_scalar(out=ncp[:], in0=ncp[:], scalar1=ncp[:, T - 1:T], scalar2=None,
                                                  op0=ALU.subtract), reads=[ncpr], writes=[ncpr])
            pb = Pool_(S, tb, "psb", [128, 512], F32, 2, psum=True)
            pc, pcr = pb.next()
            for blk in range(32):
                srct = ncp if blk < 16 else ncl
                srcr = ncpr if blk < 16 else nclr
                b16 = blk % 16
                S.op("pe", lambda e: e.transpose(out=pc[:, blk * NH:(blk + 1) * NH], in_=srct[:, b16 * 128:(b16 + 1) * 128],
                                                 identity=C.identf[0:NH, 0:NH]),
                     reads=[srcr, C.constr], writes=([pcr] if blk == 0 else []), inc=(blk == 31))
            S._put(pcr.w, (S.sem["pe"], S.cnt["pe"]))
            S.op("dve", lambda e: e.tensor_copy(out=cT[:].rearrange("p b h -> p (b h)"), in_=pc[:, 0:32 * NH]),
                 reads=[pcr], writes=[cTr])
            S.op("dve", lambda e: e.tensor_scalar(out=cT[:, 0:16, :], in0=cT[:, 0:16, :], scalar1=C.core[:, 0:1],
                                                  scalar2=None, op0=ALU.subtract), reads=[cTr, C.constr], writes=[cTr])
            pr_, prr = pb.next()
            for h in range(NH):
                S.op("pe", lambda e: e.matmul(pr_[:, h * 16:(h + 1) * 16], lhsT=sel[:, h * 128:(h + 1) * 128],
                                              rhs=ncl[:, 64:T:128], start=True, stop=True),
                     reads=[selr, nclr], writes=([prr] if h == 0 else []), inc=(h == NH - 1))
            S._put(prr.w, (S.sem["pe"], S.cnt["pe"]))
            S.op("dve", lambda e: e.tensor_copy(out=ncref[:].rearrange("p h q -> p (h q)"), in_=pr_[:, 0:NH * 16]),
                 reads=[prr], writes=[ncrr])
            for h in range(NH):
                for j in range(32):
                    S.op("dve", lambda e: e.tensor_scalar(out=biasB[:, h, j, :], in0=ncref[:, h, :],
                                                          scalar1=cT[:, j, h:h + 1], scalar2=-1.0,
                                                          op0=ALU.subtract, op1=ALU.mult),
                         reads=[ncrr, cTr], writes=[], inc=(j == 31))
                S._put(bBr.w, (S.sem["dve"], S.cnt["dve"]))
            S.barrier()
            S._put(bBr.w, (S.sem["dve"], S.cnt["dve"]))
        phase_b_main(nc, S, C, l, qaT, qbT, kv_loc, kv_all, gtab2, yT, biasB, bBr, ph)


def phase_b_main(nc, S, C, l, qaT, qbT, kv_loc, kv_all, gtab2, yT, biasB, bBr, ph):
    PAY = 4 * NH * E * T
    kaT_o = 0
    kbT_o = NH * E * T
    va_o = 2 * NH * E * T
    vb_o = 3 * NH * E * T
    sb = lambda name, shape, dtype: ph.enter_context(nc.sbuf_tensor(un(name), shape, dtype))
    psS = Pool_(S, ph, "psS", [128, 512], F32, 2, psum=True)
    psY = Pool_(S, ph, "psY", [128, 512], F32, 1, psum=True)
    psL = Pool_(S, ph, "psL", [128, 512], F32, 1, psum=True)
    psD = Pool_(S, ph, "psD", [128, 1024], F32, 1, psum=True)
    psU = Pool_(S, ph, "psU", [128, 512], F32, 1, psum=True)
    psZ = Pool_(S, ph, "psZ", [128, 512], F32, 1, psum=True)
    kTp = Pool_(S, ph, "kT", [128, 2 * T], BF16, 2)
    qTp = Pool_(S, ph, "qT", [128, T], BF16, 2)
    Vp = Pool_(S, ph, "Vf", [128, 32, 128], BF16, 1)
    PTp = Pool_(S, ph, "PT", [128, 512], BF16, 3)
    sYp = Pool_(S, ph, "sY", [128, 512], F32, 2)
    sLp = Pool_(S, ph, "sL", [128, 512], F32, 2)
    yop = Pool_(S, ph, "yo", [128, 512], BF16, 2)
    Vd = [Pool_(S, ph, f"Vd{p}", [128, DIL[p] + 16, 128], BF16, 1) for p in range(3)]
    Ecp = Pool_(S, ph, "Ec", [128, 3, 2, 2, 128], F32, 1)
    Pexp = Pool_(S, ph, "Pexp", [128, 1024], F32, 2)
    PTd = Pool_(S, ph, "PTd", [128, 1024], BF16, 2)
    accY = sb("accY", [128, T], F32)
    accL = sb("accL", [128, T], F32)
    accr = S.res("acc")
    yd = sb("yd", [128, T], BF16)
    ydr = S.res("yd")
    yTr = S.res("yTd")

    def kv_src(base, off, pattern, slot0):
        return kv_ap(kv_all if slot0 else kv_loc, base + off, pattern)

    for h in range(NH):
        kT, kTr = kTp.next()
        qT, qTr = qTp.next()
        V, Vr = Vp.next()
        S.dma("sp", [(kT[:, 0:T], kv_src(kaT_o, h * E * T, [[T, 128], [1, T]], True)),
                     (kT[:, T:2 * T], kv_src(kaT_o, h * E * T, [[T, 128], [1, T]], False))], writes=[kTr])
        S.dma("sp", [(qT[:], qaT.ap()[h])], writes=[qTr])
        prs = []
        for s0_, b0 in ((True, 0), (False, 16)):
            for hf in range(2):
                prs.append((V[:, b0 + 8 * hf:b0 + 8 * hf + 8, :],
                            kv_src(va_o, hf * CH + h * E, [[1024, 128], [128 * 1024, 8], [1, 128]], s0_)))
        S.dma("sp", prs, writes=[Vr])
        for G in range(4):
            nkb = 16 + 4 * G + 4
            pY, pYr = psY.next()
            pL, pLr = psL.next()
            for j in range(nkb):
                o = j - (16 + 4 * G)
                c0 = 128 * max(o, 0)
                pS, pSr = psS.next()
                S.op("pe", lambda e: e.matmul(pS[:, c0:512], lhsT=kT[:, j * 128:(j + 1) * 128],
                                              rhs=qT[:, G * 512 + c0:(G + 1) * 512], start=True, stop=True),
                     reads=[kTr, qTr], writes=[pSr])
                PT, PTr = PTp.next()
                nq = 0
                for qb in range(max(o, 0), 4):
                    S.op("act", lambda e: e.activation(out=PT[:, qb * 128:(qb + 1) * 128], in_=pS[:, qb * 128:(qb + 1) * 128],
                                                       func=AF.Exp, scale=SCALE, bias=biasB[:, h, j, 4 * G + qb:4 * G + qb + 1]),
                         reads=[pSr, bBr], writes=([PTr] if nq == 0 else []), inc=True)
                    nq += 1
                S._put(PTr.w, (S.sem["act"], S.cnt["act"]))
                if o >= 0:
                    S.op("dve", lambda e: e.tensor_tensor(out=PT[:, c0:c0 + 128], in0=PT[:, c0:c0 + 128], in1=C.tri[:],
                                                          op=ALU.mult), reads=[PTr, C.constr], writes=[PTr])
                first = (j == 0)
                last = (j == nkb - 1)
                S.op("pe", lambda e: e.matmul(pY[:, c0:512], lhsT=V[:, j, :], rhs=PT[:, c0:512], start=first, stop=last),
                     reads=[Vr, PTr], writes=([pYr] if (first or last) else []), inc=False)
                S.op("pe", lambda e: e.matmul(pL[:, c0:512], lhsT=C.ones[:], rhs=PT[:, c0:512], start=first, stop=last),
                     reads=[PTr, C.constr], writes=([pLr] if (first or last) else []), inc=True)
                if first or last:
                    S._put(pYr.w, (S.sem["pe"], S.cnt["pe"]))
            sY, sYr = sYp.next()
            sL, sLr = sLp.next()
            S.op("act", lambda e: e.copy(out=sY[:], in_=pY[:]), reads=[pYr], writes=[sYr])
            S.op("act", lambda e: e.copy(out=sL[:], in_=pL[:]), reads=[pLr], writes=[sLr])
            S.op("dve", lambda e: e.reciprocal(out=sL[:], in_=sL[:]), reads=[sLr], writes=[sLr])
            yo, yor = yop.next()
            S.op("dve", lambda e: e.tensor_tensor(out=yo[:], in0=sY[:], in1=sL[:], op=ALU.mult),
                 reads=[sYr, sLr], writes=[yor])
            S.dma("sp", [(yT.ap()[h, :, G * 512:(G + 1) * 512], yo[:])], reads=[yor], parts=[yTr])

        kT, kTr = kTp.next()
        qT, qTr = qTp.next()
        S.dma("sp", [(kT[:, 0:T], kv_src(kbT_o, h * E * T, [[T, 128], [1, T]], True)),
                     (kT[:, T:2 * T], kv_src(kbT_o, h * E * T, [[T, 128], [1, T]], False))], writes=[kTr])
        S.dma("sp", [(qT[:], qbT.ap()[h])], writes=[qTr])
        Ec, Ecr = Ecp.next()
        for p_ in range(3):
            S.dma("sp", [(Ec[:, p_, 0, :, :], gtab2.ap()[h, :, p_ * 256:(p_ + 1) * 256].rearrange("j (k i) -> j k i", k=2))],
                  writes=([Ecr] if p_ == 0 else []), parts=([] if p_ == 0 else [Ecr]))
        S.op("dve", lambda e: e.tensor_scalar(out=Ec[:, :, 1, 0, :], in0=Ec[:, :, 0, 0, :], scalar1=C.core[:, 1:2],
                                              scalar2=None, op0=ALU.mult), reads=[Ecr, C.constr], writes=[], inc=True)
        S.op("dve", lambda e: e.tensor_copy(out=Ec[:, :, 1, 1, :], in_=Ec[:, :, 0, 1, :]), reads=[Ecr], writes=[], inc=True)
        S._put(Ecr.w, (S.sem["dve"], S.cnt["dve"]))
        vres = []
        for p, d in enumerate(DIL):
            Vt, Vtr = Vd[p].next()
            nbn = T // (128 * d)
            prs = []
            if d == 1:
                for hf in range(2):
                    prs.append((Vt[:, 1 + 8 * hf:9 + 8 * hf, :],
                                kv_src(vb_o, hf * CH + h * E, [[1024, 128], [128 * 1024, 8], [1, 128]], False)))
                prs.append((Vt[:, 0:1, :], kv_src(vb_o, (T - 128) * 1024 + h * E, [[1024, 128], [1024, 1], [1, 128]], True)))
            elif d == 4:
                for nb_ in range(nbn):
                    prs.append((Vt[:, d + nb_ * d:d + (nb_ + 1) * d, :],
                                kv_src(vb_o, nb_ * 128 * d * 1024 + h * E, [[d * 1024, 128], [1024, d], [1, 128]], False)))
                prs.append((Vt[:, 0:d, :], kv_src(vb_o, (T - 128 * d) * 1024 + h * E, [[d * 1024, 128], [1024, d], [1, 128]], True)))
            else:
                for hf in range(2):
                    prs.append((Vt[64 * hf:64 * hf + 64, 16:32, :],
                                kv_src(vb_o, hf * CH + h * E, [[16 * 1024, 64], [1024, 16], [1, 128]], False)))
                    prs.append((Vt[64 * hf:64 * hf + 64, 0:16, :],
                                kv_src(vb_o, hf * CH + h * E, [[16 * 1024, 64], [1024, 16], [1, 128]], True)))
            S.dma("sp", prs, writes=[Vtr])
            vres.append((Vt, Vtr))
        claim(S, "dve", accr)
        claim(S, "act", accr)
        accr.w = {}
        accr.r = {}
        for p, d in enumerate(DIL):
            Vt, Vtr = vres[p]
            for bt in range(4):
                units = []
                for ub in range(4):
                    if d == 1:
                        nb, r = 4 * bt + ub, 0
                    elif d == 4:
                        nb, r = bt, ub
                    else:
                        nb, r = 0, 4 * bt + ub
                    units.append((nb, r))
                pD, pDr = psD.next()
                for ub, (nb, r) in enumerate(units):
                    q0 = nb * 128 * d + r
                    qs = qT[:, q0:q0 + 127 * d + 1:d]
                    kc_ = kT[:, T + q0:T + q0 + 127 * d + 1:d]
                    kp_ = kT[:, T + q0 - 128 * d:T + q0 - d + 1:d]
                    S.op("pe", lambda e: e.matmul(pD[:, ub * 256:ub * 256 + 128], lhsT=kp_, rhs=qs, start=True, stop=True),
                         reads=[kTr, qTr], writes=([pDr] if ub == 0 else []), inc=False)
                    S.op("pe", lambda e: e.matmul(pD[:, ub * 256 + 128:ub * 256 + 256], lhsT=kc_, rhs=qs, start=True, stop=True),
                         reads=[kTr, qTr], writes=[], inc=(ub == 3))
                S._put(pDr.w, (S.sem["pe"], S.cnt["pe"]))
                Pe, Per = Pexp.next()
                S.op("act", lambda e: e.activation(out=Pe[:, 0:512], in_=pD[:, 0:512], func=AF.Exp, scale=SCALE),
                     reads=[pDr], writes=[Per])
                S.op("act", lambda e: e.activation(out=Pe[:, 512:1024], in_=pD[:, 512:1024], func=AF.Exp, scale=SCALE),
                     reads=[pDr], writes=[])
                S._put(Per.w, (S.sem["act"], S.cnt["act"]))
                Pt, Ptr = PTd.next()
                claim(S, "dve", Ptr)
                Ptr.w = {}
                Ptr.r = {}
                for ub, (nb, r) in enumerate(units):
                    var = 1 if nb == 0 else 0
                    S.op("dve", lambda e: e.tensor_tensor(out=Pt[:, ub * 256:(ub + 1) * 256], in0=Pe[:, ub * 256:(ub + 1) * 256],
                                                          in1=Ec[:, p, var, :, :].rearrange("p k i -> p (k i)"), op=ALU.mult),
                         reads=[Per, Ecr], writes=[], inc=True)
                S._put(Ptr.w, (S.sem["dve"], S.cnt["dve"]))
                pU, pUr = psU.next()
                pZ, pZr = psZ.next()
                for ub, (nb, r) in enumerate(units):
                    sp_ = (nb) * d + r
                    sc_ = (nb + 1) * d + r
                    f_ = (ub == 0)
                    S.op("pe", lambda e: e.matmul(pU[:, ub * 128:(ub + 1) * 128], lhsT=Vt[:, sp_, :],
                                                  rhs=Pt[:, ub * 256:ub * 256 + 128], start=True, stop=False),
                         reads=[Vtr, Ptr], writes=([pUr] if f_ else []), inc=False)
                    S.op("pe", lambda e: e.matmul(pU[:, ub * 128:(ub + 1) * 128], lhsT=Vt[:, sc_, :],
                                                  rhs=Pt[:, ub * 256 + 128:ub * 256 + 256], start=False, stop=True),
                         reads=[Vtr, Ptr], writes=[], inc=False)
                    S.op("pe", lambda e: e.matmul(pZ[:, ub * 128:(ub + 1) * 128], lhsT=C.ones[:],
                                                  rhs=Pt[:, ub * 256:ub * 256 + 128], start=True, stop=False),
                         reads=[C.constr, Ptr], writes=([pZr] if f_ else []), inc=False)
                    S.op("pe", lambda e: e.matmul(pZ[:, ub * 128:(ub + 1) * 128], lhsT=C.ones[:],
                                                  rhs=Pt[:, ub * 256 + 128:ub * 256 + 256], start=False, stop=True),
                         reads=[C.constr, Ptr], writes=[], inc=(ub == 3))
                S._put(pUr.w, (S.sem["pe"], S.cnt["pe"]))
                S._put(pZr.w, (S.sem["pe"], S.cnt["pe"]))
                if d == 1:
                    oy = accY[:, bt * 512:(bt + 1) * 512]
                    ol = accL[:, bt * 512:(bt + 1) * 512]
                    iu = pU[:]
                    iz = pZ[:]
                elif d == 4:
                    oy = accY[:, bt * 512:(bt + 1) * 512].rearrange("p (i r) -> p r i", r=4)
                    ol = accL[:, bt * 512:(bt + 1) * 512].rearrange("p (i r) -> p r i", r=4)
                    iu = pU[:].rearrange("p (r i) -> p r i", r=4)
                    iz = pZ[:].rearrange("p (r i) -> p r i", r=4)
                else:
                    oy = accY[:].rearrange("p (i r) -> p r i", r=16)[:, 4 * bt:4 * bt + 4, :]
                    ol = accL[:].rearrange("p (i r) -> p r i", r=16)[:, 4 * bt:4 * bt + 4, :]
                    iu = pU[:].rearrange("p (r i) -> p r i", r=4)
                    iz = pZ[:].rearrange("p (r i) -> p r i", r=4)
                if p == 0:
                    S.op("act", lambda e: e.copy(out=oy, in_=iu), reads=[pUr], writes=[], inc=True)
                    S.op("act", lambda e: e.copy(out=ol, in_=iz), reads=[pZr], writes=[], inc=True)
                    S._put(accr.w, (S.sem["act"], S.cnt["act"]))
                else:
                    S.op("dve", lambda e: e.tensor_tensor(out=oy, in0=iu, in1=oy, op=ALU.add), reads=[pUr, accr], writes=[], inc=True)
                    S.op("dve", lambda e: e.tensor_tensor(out=ol, in0=iz, in1=ol, op=ALU.add), reads=[pZr, accr], writes=[], inc=True)
                    S._put(accr.w, (S.sem["dve"], S.cnt["dve"]))
        S.op("dve", lambda e: e.reciprocal(out=accL[:], in_=accL[:]), reads=[accr], writes=[accr])
        S.op("dve", lambda e: e.tensor_tensor(out=yd[:], in0=accY[:], in1=accL[:], op=ALU.mult), reads=[accr], writes=[ydr])
        S.dma("sp", [(yT.ap()[NH + h], yd[:])], reads=[ydr], parts=[yTr])
    S.barrier()

def phase_c(nc, S, C, l, xsrc, xdst, x1, yT, on_a, on_b, w_out, norm2_g, w1, w2):
    with ExitStack() as ph:
        sb = lambda name, shape, dtype: ph.enter_context(nc.sbuf_tensor(un(name), shape, dtype))
        C.stat = Pool_(S, ph, "stat", [128, 4], F32, 4)
        C.hb = Pool_(S, ph, "hb", [128, D], BF16, 2)
        C.junk = C.hb
        C.pst = Pool_(S, ph, "pst", [128, 512], F32, 2, psum=True)
        pss = Pool_(S, ph, "pss", [128, 512], F32, 2, psum=True)
        psm = Pool_(S, ph, "psm", [128, 512], F32, 4, psum=True)
        xrow = Pool_(S, ph, "xrow", [128, D], F32, 2)
        xpc = Pool_(S, ph, "xpc", [128, 512], F32, 4)
        abp = Pool_(S, ph, "ab", [128, 16, 512], BF16, 2)
        wpool = Pool_(S, ph, "wt", [128, 16, 512], BF16, 3)
        sqp = Pool_(S, ph, "sq", [128, 512], BF16, 2)
        rlp = Pool_(S, ph, "rl", [128, 512], F32, 2)
        h1T = sb("h1T", [128, 64, 512], BF16)
        h1r = S.res("h1T")
        rs = sb("rs", [128, 2, 512], F32)
        rsg = [S.res("rs0"), S.res("rs1")]
        gbc = sb("gbc", [128, D], F32)
        gr = S.res("gbc")
        gcol = sb("gcol", [128, 16], F32)
        gcr = S.res("gcol")
        S.dma("sp", [(gbc[:], dram_ap(norm2_g, l * D, [[0, 128], [1, D]]))], writes=[gr])
        S.dma("sp", [(gcol[:, 0:8], dram_ap(on_a, l * 1024, [[1, 128], [128, 8]])),
                     (gcol[:, 8:16], dram_ap(on_b, l * 1024, [[1, 128], [128, 8]]))], writes=[gcr],
              allow_slow_non_contiguous=True)
        wo = w_out.ap()[l]
        w1l = w1.ap()[l]
        w2l = w2.ap()[l]
        x1r = S.res("x1d")
        xdr = S.res("xdst")

        for rt in range(4):
            t0 = rt * 512
            A, Ar = abp.next()
            S.dma("sp", [(A[:], yT.ap()[:, :, t0:t0 + 512].rearrange("h e t -> e h t"))], writes=[Ar])
            for grp in range(2):
                pq, pqr = pss.next()
                for hh in range(8):
                    sq, sqr = sqp.next()
                    S.op("dve", lambda e: e.tensor_tensor(out=sq[:], in0=A[:, grp * 8 + hh, :], in1=A[:, grp * 8 + hh, :],
                                                          op=ALU.mult), reads=[Ar], writes=[sqr])
                    S.op("pe", lambda e: e.matmul(pq[:], lhsT=C.ones[:], rhs=sq[:], start=(hh == 0), stop=(hh == 7)),
                         reads=[sqr, C.constr], writes=([pqr] if (hh == 0 or hh == 7) else []), inc=True)
                S.op("act", lambda e: e.activation(out=rs[:, grp, :], in_=pq[:], func=AF.Ln, scale=1.0 / 1024,
                                                   bias=C.eps_t[:, 0:1]), reads=[pqr, C.constr], writes=[rsg[grp]])
                S.op("act", lambda e: e.activation(out=rs[:, grp, :], in_=rs[:, grp, :], func=AF.Exp, scale=-0.5),
                     reads=[rsg[grp]], writes=[rsg[grp]])
            for hh in range(16):
                grp = hh // 8
                S.op("dve", lambda e: e.scalar_tensor_tensor(out=A[:, hh, :], in0=A[:, hh, :], scalar=gcol[:, hh:hh + 1],
                                                             in1=rs[:, grp, :], op0=ALU.mult, op1=ALU.mult),
                     reads=[rsg[grp], gcr, Ar], writes=[])
            S._put(Ar.w, (S.sem["dve"], S.cnt["dve"]))
            for cg in range(4):
                wt, wr = load_wtile(S, wpool, wo, 0, cg * 512)
                for st in range(4):
                    xp, xpr = xpc.next()
                    S.dma("sp", [(xp[:], xsrc.ap()[t0 + st * 128:t0 + (st + 1) * 128, cg * 512:(cg + 1) * 512])],
                          writes=[xpr])
                    pm, pmr = psm.next()
                    for kc in range(16):
                        S.op("pe", lambda e, kc=kc: e.matmul(pm[:], lhsT=A[:, kc, st * 128:(st + 1) * 128],
                                                             rhs=wt[:, kc, :], start=(kc == 0), stop=(kc == 15)),
                             reads=[wr, Ar], writes=([pmr] if kc == 0 else []), inc=(kc == 15))
                    S.op("dve", lambda e: e.tensor_tensor(out=xp[:], in0=pm[:], in1=xp[:], op=ALU.add),
                         reads=[pmr], writes=[xpr])
                    S.dma("sp", [(x1.ap()[t0 + st * 128:t0 + (st + 1) * 128, cg * 512:(cg + 1) * 512], xp[:])],
                          reads=[xpr], parts=[x1r])
            B, Br = abp.next()
            claim(S, "act", Br)
            claim(S, "dve", Br)
            Br.w = {}
            Br.r = {}
            for st in range(4):
                xt, xr = xrow.next()
                S.dma("sp", [(xt[:], x1.ap()[t0 + st * 128:t0 + (st + 1) * 128, :])], reads=[x1r], writes=[xr])
                rmsnorm_to_hT(S, C, xt, xr, gbc, gr, B, Br, st)
            claim(S, "dve", h1r)
            h1r.w = {}
            h1r.r = {}
            for ft in range(16):
                wt, wr = load_wtile(S, wpool, w1l, 0, ft * 512)
                for fb in range(4):
                    pm, pmr = psm.next()
                    for kc in range(16):
                        S.op("pe", lambda e, kc=kc: e.matmul(pm[:], lhsT=wt[:, kc, fb * 128:(fb + 1) * 128],
                                                             rhs=B[:, kc, :], start=(kc == 0), stop=(kc == 15)),
                             reads=[wr, Br], writes=([pmr] if kc == 0 else []), inc=(kc == 15))
                    rl, rlr = rlp.next()
                    S.op("act", lambda e: e.activation(out=rl[:], in_=pm[:], func=AF.Relu), reads=[pmr], writes=[rlr])
                    S.op("dve", lambda e: e.tensor_tensor(out=h1T[:, ft * 4 + fb, :], in0=rl[:], in1=rl[:], op=ALU.mult),
                         reads=[rlr], writes=[])
                    S._put(h1r.w, (S.sem["dve"], S.cnt["dve"]))
            for cg in range(4):
                banks = [psm.next() for _ in range(4)]
                for q in range(4):
                    wt, wr = load_wtile(S, wpool, w2l, q * 2048, cg * 512)
                    for st in range(4):
                        pm, pmr = banks[st]
                        for kc in range(16):
                            first = (q == 0 and kc == 0)
                            last = (q == 3 and kc == 15)
                            S.op("pe", lambda e, kc=kc: e.matmul(pm[:], lhsT=h1T[:, q * 16 + kc, st * 128:(st + 1) * 128],
                                                                 rhs=wt[:, kc, :], start=first, stop=last),
                                 reads=[wr, h1r], writes=([pmr] if (first or last) else []), inc=(kc == 15))
                for st in range(4):
                    pm, pmr = banks[st]
                    xp, xpr = xpc.next()
                    S.dma("sp", [(xp[:], x1.ap()[t0 + st * 128:t0 + (st + 1) * 128, cg * 512:(cg + 1) * 512])],
                          reads=[x1r], writes=[xpr])
                    S.op("dve", lambda e: e.tensor_tensor(out=xp[:], in0=pm[:], in1=xp[:], op=ALU.add),
                         reads=[pmr], writes=[xpr])
                    S.dma("sp", [(xdst.ap()[t0 + st * 128:t0 + (st + 1) * 128, cg * 512:(cg + 1) * 512], xp[:])],
                          reads=[xpr], parts=[xdr])
        S.barrier()


def phase_final(nc, S, C, x2, fin_g, y_out):
    with ExitStack() as ph:
        sb = lambda name, shape, dtype: ph.enter_context(nc.sbuf_tensor(un(name), shape, dtype))
        C.stat = Pool_(S, ph, "stat", [128, 4], F32, 4)
        C.junk = Pool_(S, ph, "junk", [128, D], BF16, 2)
        xrow = Pool_(S, ph, "xrow", [128, D], F32, 3)
        gbc = sb("gbc", [128, D], F32)
        gr = S.res("gbc")
        yr = S.res("yout")
        S.dma("sp", [(gbc[:], dram_ap(fin_g, 0, [[0, 128], [1, D]]))], writes=[gr])
        for i in range(16):
            xt, xr = xrow.next()
            S.dma("sp", [(xt[:], x2.ap()[i * 128:(i + 1) * 128, :])], writes=[xr])
            rmsnorm_to_hT(S, C, xt, xr, gbc, gr, None, None, 0, out_norm=(xt, xr))
            S.dma("sp", [(y_out.ap()[i * 128:(i + 1) * 128, :], xt[:])], reads=[xr], parts=[yr])
        S.barrier()

def host_consts():
    ident = np.eye(128, dtype=np.float32)
    s = np.arange(128)
    tri = (s[:, None] <= s[None, :]).astype(np.float32)
    sel = np.zeros((8, 8, 128), np.float32)
    for h in range(8):
        sel[h, h, :] = 1.0
    oh = np.zeros((33, 3, 384), np.float32)
    for p, d in enumerate(DIL):
        for xx in range(384):
            rel = xx - 128
            if rel < 0 or rel > 128:
                oh[32, p, xx] = 1.0
                continue
            dist = rel * d
            if dist < 16:
                b = dist
            else:
                dd = np.float32(max(dist, 1))
                v = np.log(dd / np.float32(16)) / np.float32(np.log(2048 / 16)) * np.float32(16)
                b = min(16 + int(np.float32(v)), 31)
            oh[b, p, xx] = 1.0
    return {
        "c_ident": ident.astype(ml_dtypes.bfloat16),
        "c_tri": tri.astype(ml_dtypes.bfloat16),
        "c_identf": ident,
        "c_anti": np.ascontiguousarray(ident[::-1]),
        "c_sel": sel.reshape(8, 8 * 128),
        "c_oh": oh.reshape(33, 3 * 384),
    }


_NC_CACHE = {}


def make_in_maps(inputs):
    consts = host_consts()
    x = np.asarray(inputs["x"], dtype=np.float32)
    shared = {k: np.ascontiguousarray(np.asarray(v, dtype=np.float32)) for k, v in inputs.items() if k != "x"}
    in_maps = []
    for c in range(NCORES):
        b, half = c // 2, c % 2
        m = dict(shared)
        m.update(consts)
        m["x"] = np.ascontiguousarray(x[b, half * T:(half + 1) * T, :])
        core = np.zeros((128, 2), np.float32)
        core[:, 0] = 30000.0 if half == 0 else 0.0
        core[:, 1] = 0.0 if half == 0 else 1.0
        m["c_core"] = core
        in_maps.append(m)
    return in_maps


def kernel(**inputs):
    if "nc" not in _NC_CACHE:
        _NC_CACHE["nc"] = build()
    nc = _NC_CACHE["nc"]
    in_maps = make_in_maps(inputs)
    res = run_bass_kernel_spmd(nc, in_maps, core_ids=list(range(NCORES)))
    out = np.zeros((4, SEQ, D), np.float32)
    for c in range(NCORES):
        b, half = c // 2, c % 2
        out[b, half * T:(half + 1) * T, :] = np.asarray(res.results[c]["y"], dtype=np.float32)
    return out
```

```python
import numpy as np
import ml_dtypes
from contextlib import ExitStack
import concourse.bass as bass
import concourse.mybir as mybir
from concourse.bass_utils import run_bass_kernel_spmd

F32 = mybir.dt.float32
BF16 = mybir.dt.bfloat16
AF = mybir.ActivationFunctionType
ALU = mybir.AluOpType

D = 2048
T = 2048
SEQ = 4096
NH = 8
E = 128
NIN = 6152
DFF = 8192
DEPTH = 2
EPS = 1e-6
SCALE = E ** -0.5
NEG = -30000.0
DIL = (1, 4, 16)
NCORES = 8


_UID = [0]


def un(name):
    _UID[0] += 1
    return f"{name}_{_UID[0]}"


class Res:
    __slots__ = ("name", "w", "r", "dsem", "dcnt", "keep")

    def __init__(self, name):
        self.name = name
        self.w = {}
        self.r = {}
        self.dsem = None
        self.dcnt = 0
        self.keep = False


class Sync:
    def __init__(self, nc, stack):
        self.nc = nc
        self.stack = stack
        self.eng = {"pe": nc.tensor, "act": nc.scalar, "dve": nc.vector, "pool": nc.gpsimd, "sp": nc.sync}
        self.sem = {}
        self.cnt = {}
        self.known = {}
        self.pending = {}
        for e in self.eng:
            self.sem[e] = stack.enter_context(nc.semaphore("s_" + e))
            self.cnt[e] = 0
            self.known[e] = {}
            self.pending[e] = False
        self.nsem = 5
        self.all_res = []
        self.dstack = stack
        self.semval = {}

    def res(self, name):
        r = Res(un(name))
        self.all_res.append(r)
        return r

    def _wait(self, e, evs):
        kn = self.known[e]
        for (sem, val) in evs:
            key = sem.num
            if kn.get(key, 0) < val:
                self.eng[e].wait_ge(sem, val)
                kn[key] = val

    def _deps(self, reads, writes, parts=()):
        evs = []
        for r in reads:
            evs.extend(r.w.values())
        for w in writes:
            evs.extend(w.w.values())
            evs.extend(w.r.values())
        for w in parts:
            evs.extend(w.r.values())
        return evs

    @staticmethod
    def _put(d, ev):
        k = ev[0].num
        if k not in d or d[k][1] < ev[1]:
            d[k] = ev

    def _record(self, ev, reads, writes, parts=()):
        for r in reads:
            self._put(r.r, ev)
        for w in writes:
            w.w = {ev[0].num: ev}
            w.r = {}
        for w in parts:
            self._put(w.w, ev)

    def op(self, e, fn, reads=(), writes=(), inc=True):
        self._wait(e, self._deps(reads, writes))
        inst = fn(self.eng[e])
        ev = (self.sem[e], self.cnt[e] + 1)
        if inc:
            self.cnt[e] += 1
            inst.then_inc(self.sem[e], 1)
            self.pending[e] = False
        else:
            self.pending[e] = True
        self._record(ev, reads, writes)
        return inst

    def dsem_of(self, r):
        if r.dsem is None:
            r.dsem = self.dstack.enter_context(self.nc.semaphore("d_" + r.name))
            r.dcnt = self.semval.get(r.dsem.num, 0)
            self.nsem += 1
        return r.dsem

    def dma(self, q, pairs, reads=(), writes=(), parts=(), semres=None, **kw):
        self._wait(q, self._deps(reads, writes, parts))
        sr = semres if semres is not None else (writes[0] if writes else (parts[0] if parts else reads[0]))
        sem = self.dsem_of(sr)
        for (o, i) in pairs:
            self.eng[q].dma_start(out=o, in_=i, **kw).then_inc(sem, 16)
            sr.dcnt += 16
        self.semval[sem.num] = sr.dcnt
        ev = (sem, sr.dcnt)
        self._record(ev, reads, writes, parts)
        return ev

    def barrier(self):
        evs = []
        for e in self.eng:
            assert not self.pending[e], e
            if self.cnt[e]:
                evs.append((self.sem[e], self.cnt[e]))
        for r in self.all_res:
            if r.dsem is not None and r.dcnt:
                evs.append((r.dsem, r.dcnt))
        for e in self.eng:
            self._wait(e, evs)
        for r in self.all_res:
            r.w = {}
            r.r = {}

    def begin_phase(self, ph):
        self.dstack = ph
        self._mark = len(self.all_res)

    def end_phase(self):
        del self.all_res[self._mark:]
        self.dstack = self.stack


class Pool_:
    def __init__(self, S, stack, name, shape, dtype, n, psum=False):
        self.bufs = []
        for i in range(n):
            if psum:
                t = stack.enter_context(S.nc.psum_tensor(un(f"{name}{i}"), shape, dtype))
            else:
                t = stack.enter_context(S.nc.sbuf_tensor(un(f"{name}{i}"), shape, dtype))
            self.bufs.append((t, S.res(f"{name}{i}")))
        self.i = 0

    def next(self):
        b = self.bufs[self.i % len(self.bufs)]
        self.i += 1
        return b


CH = 1024 * 1024


def kv_ap(tensors, off, pattern):
    chunk, loc = off // CH, off % CH
    extent = sum((c - 1) * s for s, c in pattern)
    assert loc + extent < CH, (off, pattern)
    return dram_ap(tensors[chunk], loc, pattern)


def dram_ap(t, offset, pattern):
    return bass.AP(tensor=t, offset=offset, ap=[list(p) for p in pattern])


class Ctx:
    pass


def load_wtile(S, wpool, wdram_ap2d, k0, c0, ncols=512):
    wt, wr = wpool.next()
    src = wdram_ap2d[k0:k0 + 2048, c0:c0 + ncols].rearrange("(kc p) c -> p kc c", p=128)
    S.dma("pool", [(wt[:, :, 0:ncols], src)], writes=[wr])
    return wt, wr


def rmsnorm_to_hT(S, C, xt, xr, gbc, gr, hT, hTr, st, out_norm=None):
    junk, junkr = C.junk.next()
    ss, ssr = C.stat.next()
    S.op("act", lambda e: e.activation(out=junk[:], in_=xt[:], func=AF.Square, accum_out=ss[:, 0:1]),
         reads=[xr], writes=[junkr, ssr])
    S.op("act", lambda e: e.activation(out=ss[:, 1:2], in_=ss[:, 0:1], func=AF.Ln, scale=1.0 / D, bias=C.eps_t[:, 0:1]),
         reads=[ssr, C.constr], writes=[ssr])
    S.op("act", lambda e: e.activation(out=ss[:, 2:3], in_=ss[:, 1:2], func=AF.Exp, scale=-0.5),
         reads=[ssr], writes=[ssr])
    if out_norm is not None:
        ot, otr = out_norm
        S.op("dve", lambda e: e.scalar_tensor_tensor(out=ot[:], in0=xt[:], scalar=ss[:, 2:3], in1=gbc[:],
                                                     op0=ALU.mult, op1=ALU.mult),
             reads=[xr, ssr, gr], writes=[otr])
        return
    hb, hbr = C.hb.next()
    S.op("dve", lambda e: e.scalar_tensor_tensor(out=hb[:], in0=xt[:], scalar=ss[:, 2:3], in1=gbc[:],
                                                 op0=ALU.mult, op1=ALU.mult),
         reads=[xr, ssr, gr], writes=[hbr])
    transpose_rows(S, C, hb, hbr, hT, hTr, st)


def transpose_rows(S, C, hb, hbr, hT, hTr, st):
    for g in range(4):
        pt, ptr = C.pst.next()
        ptb = pt[:].bitcast(BF16)
        for j in range(4):
            kc = g * 4 + j
            S.op("pe", lambda e, kc=kc, j=j: e.transpose(out=ptb[:, j * 128:(j + 1) * 128],
                                                         in_=hb[:, kc * 128:(kc + 1) * 128], identity=C.ident[:]),
                 reads=[hbr, C.constr], writes=([ptr] if j == 0 else []), inc=(j == 3))
        src = ptb[:, 0:512].rearrange("p (j t) -> p j t", j=4)
        dst = hT[:, g * 4:(g + 1) * 4, st * 128:(st + 1) * 128]
        eng = "act" if (g % 2 == 0) else "dve"
        if eng == "act":
            S.op("act", lambda e: e.copy(out=dst, in_=src), reads=[ptr], writes=[])
        else:
            S.op("dve", lambda e: e.tensor_copy(out=dst, in_=src), reads=[ptr], writes=[])
        ev = (S.sem[eng], S.cnt[eng])
        S._put(hTr.w, ev)


def claim(S, e, res):
    S._wait(e, list(res.w.values()) + list(res.r.values()))


def build(debug_phase=None):
    nc = bass.Bass("TRN2", target_bir_lowering=False)
    dt = nc.dram_tensor
    x_in = dt("x", [T, D], F32, kind="ExternalInput")
    norm1_g = dt("norm1_g", [DEPTH, D], F32, kind="ExternalInput")
    w_in = dt("w_in", [DEPTH, D, NIN], F32, kind="ExternalInput")
    forget_b = dt("forget_b", [DEPTH, NH], F32, kind="ExternalInput")
    rel_bias = dt("rel_bias", [32, NH], F32, kind="ExternalInput")
    full = debug_phase in (None, "c")
    if full:
        on_a = dt("outnorm_a_g", [DEPTH, 1024], F32, kind="ExternalInput")
        on_b = dt("outnorm_b_g", [DEPTH, 1024], F32, kind="ExternalInput")
        w_out = dt("w_out", [DEPTH, D, D], F32, kind="ExternalInput")
        norm2_g = dt("norm2_g", [DEPTH, D], F32, kind="ExternalInput")
        w1 = dt("w_mlp_in", [DEPTH, D, DFF], F32, kind="ExternalInput")
        w2 = dt("w_mlp_out", [DEPTH, DFF, D], F32, kind="ExternalInput")
        fin_g = dt("final_norm_g", [D], F32, kind="ExternalInput")
    else:
        on_a = on_b = w_out = norm2_g = w1 = w2 = fin_g = None
    c_ident = dt("c_ident", [128, 128], BF16, kind="ExternalInput")
    c_tri = dt("c_tri", [128, 128], BF16, kind="ExternalInput")
    c_identf = dt("c_identf", [128, 128], F32, kind="ExternalInput")
    c_sel = dt("c_sel", [8, 8 * 128], F32, kind="ExternalInput")
    c_oh = dt("c_oh", [33, 3 * 384], F32, kind="ExternalInput")
    c_anti = dt("c_anti", [128, 128], F32, kind="ExternalInput")
    c_core = dt("c_core", [128, 2], F32, kind="ExternalInput")
    y_out = dt("y", [T, D], F32, kind="ExternalOutput")
    PAY = 4 * NH * E * T
    kv_loc = [[dt(f"kv_loc{l}_{i}", [1024, 1024], BF16) for i in range(8)] for l in range(DEPTH)]
    kv_all = [[dt(f"kv_all{l}_{i}", [2048, 1024], BF16) for i in range(8)] for l in range(DEPTH)]
    c_loc = [dt(f"c_loc{l}", [NH, T], F32) for l in range(DEPTH)]
    c_all = [dt(f"c_all{l}", [2 * NH, T], F32) for l in range(DEPTH)]
    qaT = dt("qaT", [NH, E, T], BF16)
    qbT = dt("qbT", [NH, E, T], BF16)
    yT = dt("yT", [2 * NH, E, T], BF16)
    x1 = dt("x1", [T, D], F32)
    x2 = dt("x2", [T, D], F32)
    x3 = dt("x3", [T, D], F32)
    gtab = dt("gtab", [NH, 3 * 384], F32)
    gtab2 = dt("gtab2", [NH, 128, 768], F32)

    dbg = {}
    if debug_phase is not None:
        dbg["qaT"] = dt("dbg_qaT", [NH, E, T], BF16, kind="ExternalOutput")
        dbg["kv"] = dt("dbg_kv", [PAY], BF16, kind="ExternalOutput")
        dbg["c"] = dt("dbg_c", [NH * T], F32, kind="ExternalOutput")
        dbg["qbT"] = dt("dbg_qbT", [NH, E, T], BF16, kind="ExternalOutput")
        dbg["yT"] = dt("dbg_yT", [2 * NH, E, T], BF16, kind="ExternalOutput")
        dbg["x2"] = dt("dbg_x2", [T, D], F32, kind="ExternalOutput")

    with ExitStack() as top:
        S = Sync(nc, top)
        C = Ctx()
        C.nc = nc
        sb = lambda name, shape, dtype: top.enter_context(nc.sbuf_tensor(un(name), shape, dtype))
        C.ident = sb("ident", [128, 128], BF16)
        C.tri = sb("tri", [128, 128], BF16)
        C.identf = sb("identf", [128, 128], F32)
        C.ones = sb("ones", [128, 128], BF16)
        C.eps_t = sb("eps_t", [128, 1], F32)
        C.core = sb("corec", [128, 2], F32)
        C.constr = S.res("const")
        S.dma("sp", [(C.ident[:], c_ident.ap()), (C.tri[:], c_tri.ap()), (C.identf[:], c_identf.ap()),
                     (C.core[:], c_core.ap())], writes=[C.constr])
        S.op("dve", lambda e: e.memset(C.ones[:], 1.0), writes=[])
        S.op("dve", lambda e: e.memset(C.eps_t[:], EPS), writes=[])
        S._put(C.constr.w, (S.sem["dve"], S.cnt["dve"]))

        ccsem = top.enter_context(nc.semaphore("ccsem"))
        cccnt = [0]
        if debug_phase != "a":
            phase_setup(nc, S, C, rel_bias, c_oh, c_anti, gtab, gtab2)
        for l in range(DEPTH):
            xsrc = x_in if l == 0 else x2
            xdst = x2 if l == 0 else x3
            phase_a(nc, S, C, l, xsrc, norm1_g, w_in, forget_b, qaT, qbT, kv_loc[l], c_loc[l])
            if debug_phase == "a":
                break
            if debug_phase == "s":
                break
            phase_b(nc, S, C, l, qaT, qbT, kv_loc[l], kv_all[l], c_loc[l], c_all[l], c_sel, gtab2, yT, ccsem, cccnt)
            if debug_phase == "b":
                break
            phase_c(nc, S, C, l, xsrc, xdst, x1, yT, on_a, on_b, w_out, norm2_g, w1, w2)
            if debug_phase == "c":
                break
        if debug_phase in ("b", "c", "s"):
            with ExitStack() as st:
                tb = st.enter_context(nc.sbuf_tensor("dbgbufy", [128, 16 * 2048], BF16))
                tr = S.res("dbgbufy")
                S.dma("sp", [(tb[:].rearrange("e (h t) -> e h t", h=16), yT.ap().rearrange("h e t -> e h t"))], writes=[tr])
                S.dma("sp", [(dbg["yT"].ap().rearrange("h e t -> e h t"), tb[:].rearrange("e (h t) -> e h t", h=16))], reads=[tr])
                tx = st.enter_context(nc.sbuf_tensor("dbgbufx", [128, 16 * 2048], F32))
                txr = S.res("dbgbufx")
                S.dma("sp", [(tx[:].rearrange("p (n d) -> p n d", n=16), x2.ap().rearrange("(n p) d -> p n d", p=128))], writes=[txr])
                S.dma("sp", [(dbg["x2"].ap().rearrange("(n p) d -> p n d", p=128), tx[:].rearrange("p (n d) -> p n d", n=16))], reads=[txr])
                S.barrier()
        if debug_phase is None:
            phase_final(nc, S, C, x3, fin_g, y_out)
        S.barrier()
    return nc


def phase_a(nc, S, C, l, xsrc, norm1_g, w_in, forget_b, qaT, qbT, kv_loc, c_loc):
    with ExitStack() as ph:
        S.begin_phase(ph)
        sb = lambda name, shape, dtype: ph.enter_context(nc.sbuf_tensor(un(name), shape, dtype))
        C.junk = Pool_(S, ph, "junk", [128, D], BF16, 1)
        C.stat = Pool_(S, ph, "stat", [128, 4], F32, 4)
        C.hb = Pool_(S, ph, "hb", [128, D], BF16, 2)
        C.pst = Pool_(S, ph, "pst", [128, 512], F32, 2, psum=True)
        xpool = Pool_(S, ph, "xt", [128, D], F32, 3)
        wpool = Pool_(S, ph, "wt", [128, 16, 512], BF16, 3)
        hTp = Pool_(S, ph, "hT", [128, 16, 512], BF16, 2)
        psm = Pool_(S, ph, "psm", [128, 512], F32, 4, psum=True)
        psf = Pool_(S, ph, "psf", [128, 512], F32, 1, psum=True)
        stg = Pool_(S, ph, "stg", [128, 512], BF16, 4)
        gbc = sb("gbc", [128, D], F32)
        gr = S.res("gbc")
        wf = sb("wf", [128, 16, 8], BF16)
        wfr = S.res("wf")
        zf = sb("zf", [8, T], F32)
        zfr = S.res("zf")
        nb = sb("nb", [8, 2], F32)
        nbr = S.res("nb")
        zeros = sb("zeros", [8, T], F32)
        zr = S.res("zeros")

        S.dma("sp", [(gbc[:], dram_ap(norm1_g, l * D, [[0, 128], [1, D]]))], writes=[gr])
        wl = w_in.ap()[l]
        S.dma("pool", [(wf[:], wl[:, 3072:3080].rearrange("(kc p) c -> p kc c", p=128))], writes=[wfr])
        S.dma("sp", [(nb[:, 0:1], dram_ap(forget_b, l * NH, [[1, NH], [1, 1]]))], writes=[nbr])
        S.op("dve", lambda e: e.tensor_scalar(out=nb[:, 1:2], in0=nb[:, 0:1], scalar1=-1.0, scalar2=None, op0=ALU.mult),
             reads=[nbr], writes=[nbr])
        S.op("dve", lambda e: e.memset(zeros[:], 0.0), writes=[zr])

        kaT_o = 0
        kbT_o = NH * E * T
        va_o = 2 * NH * E * T
        vb_o = 3 * NH * E * T
        kvr = S.res("kvloc")
        qr = S.res("qdram")

        tiles = []
        for j in range(2):
            tiles.append((0 + 512 * j, "fm", ("qa", 4 * j)))
        for j in range(2):
            tiles.append((1024 + 512 * j, "fm", ("ka", 4 * j)))
        for j in range(2):
            tiles.append((2048 + 512 * j, "tm", ("va", 512 * j)))
        for j in range(2):
            tiles.append((3080 + 512 * j, "fm", ("qb", 4 * j)))
        for j in range(2):
            tiles.append((4104 + 512 * j, "fm", ("kb", 4 * j)))
        for j in range(2):
            tiles.append((5128 + 512 * j, "tm", ("vb", 512 * j)))

        for rt in range(4):
            t0 = rt * 512
            hT, hTr = hTp.next()
            claim(S, "act", hTr)
            claim(S, "dve", hTr)
            hTr.w = {}
            hTr.r = {}
            for st in range(4):
                xt, xr = xpool.next()
                S.dma("sp", [(xt[:], xsrc.ap()[t0 + st * 128:t0 + (st + 1) * 128, :])], writes=[xr])
                rmsnorm_to_hT(S, C, xt, xr, gbc, gr, hT, hTr, st)
            pf, pfr = psf.next()
            for kc in range(16):
                S.op("pe", lambda e, kc=kc: e.matmul(pf[0:8, :], lhsT=wf[:, kc, :], rhs=hT[:, kc, :],
                                                     start=(kc == 0), stop=(kc == 15)),
                     reads=[wfr, hTr], writes=([pfr] if kc == 0 else []), inc=(kc == 15))
            S.op("dve", lambda e: e.tensor_copy(out=zf[:, t0:t0 + 512], in_=pf[0:8, :]), reads=[pfr], writes=[])
            S._put(zfr.w, (S.sem["dve"], S.cnt["dve"]))
            for (c0, kind, dst) in tiles:
                wt, wr = load_wtile(S, wpool, wl, 0, c0)
                if kind == "fm":
                    name, h0 = dst
                    for hb_ in range(4):
                        pm, pmr = psm.next()
                        for kc in range(16):
                            S.op("pe", lambda e, kc=kc: e.matmul(pm[:], lhsT=wt[:, kc, hb_ * 128:(hb_ + 1) * 128],
                                                                 rhs=hT[:, kc, :], start=(kc == 0), stop=(kc == 15)),
                                 reads=[wr, hTr], writes=([pmr] if kc == 0 else []), inc=(kc == 15))
                        sg, sgr = stg.next()
                        S.op("act", lambda e: e.copy(out=sg[:], in_=pm[:]), reads=[pmr], writes=[sgr])
                        h = h0 + hb_
                        if name == "qa":
                            S.dma("sp", [(qaT.ap()[h, :, t0:t0 + 512], sg[:])], reads=[sgr], parts=[qr])
                        elif name == "qb":
                            S.dma("sp", [(qbT.ap()[h, :, t0:t0 + 512], sg[:])], reads=[sgr], parts=[qr])
                        else:
                            base = kaT_o if name == "ka" else kbT_o
                            dstap = kv_ap(kv_loc, base + h * E * T + t0, [[T, 128], [1, 512]])
                            S.dma("sp", [(dstap, sg[:])], reads=[sgr], parts=[kvr])
                else:
                    name, cc0 = dst
                    base = va_o if name == "va" else vb_o
                    for st in range(4):
                        pm, pmr = psm.next()
                        for kc in range(16):
                            S.op("pe", lambda e, kc=kc: e.matmul(pm[:], lhsT=hT[:, kc, st * 128:(st + 1) * 128],
                                                                 rhs=wt[:, kc, :], start=(kc == 0), stop=(kc == 15)),
                                 reads=[wr, hTr], writes=([pmr] if kc == 0 else []), inc=(kc == 15))
                        sg, sgr = stg.next()
                        S.op("dve", lambda e: e.tensor_copy(out=sg[:], in_=pm[:]), reads=[pmr], writes=[sgr])
                        dstap = kv_ap(kv_loc, base + (t0 + st * 128) * 1024 + cc0, [[1024, 128], [1, 512]])
                        S.dma("sp", [(dstap, sg[:])], reads=[sgr], parts=[kvr])
        ef = sb("ef", [8, T], F32)
        efr = S.res("ef")
        S.op("act", lambda e: e.activation(out=ef[:], in_=zf[:], func=AF.Exp, scale=-1.0, bias=nb[:, 1:2]),
             reads=[zfr, nbr], writes=[efr])
        S.op("act", lambda e: e.activation(out=ef[:], in_=ef[:], func=AF.Ln, scale=1.0, bias=1.0),
             reads=[efr], writes=[efr])
        S.op("dve", lambda e: e.tensor_tensor_scan(out=zf[:], data0=ef[:], data1=zeros[:], initial=0.0,
                                                   op0=ALU.add, op1=ALU.add),
             reads=[efr, zr], writes=[zfr])
        S.dma("sp", [(c_loc.ap(), zf[:])], reads=[zfr], parts=[kvr])
        S.barrier()
        S.end_phase()


def phase_setup(nc, S, C, rel_bias, c_oh, c_anti, gtab, gtab2):
    with ExitStack() as ph:
        S.begin_phase(ph)
        sb = lambda name, shape, dtype: ph.enter_context(nc.sbuf_tensor(un(name), shape, dtype))
        rb = sb("rbext", [33, NH], F32)
        rbr = S.res("rbext")
        oh = sb("oh", [33, 3 * 384], F32)
        ohr = S.res("oh")
        gt = sb("gt", [NH, 3 * 384], F32)
        gtr = S.res("gt")
        pp = Pool_(S, ph, "psg", [128, 512], F32, 3, psum=True)
        S.op("dve", lambda e: e.memset(rb[32:33, :], NEG), writes=[rbr])
        S.dma("sp", [(rb[0:32, :], rel_bias.ap())], parts=[rbr])
        S.dma("sp", [(oh[:], c_oh.ap())], writes=[ohr])
        for p in range(3):
            pg, pgr = pp.next()
            S.op("pe", lambda e: e.matmul(pg[0:NH, 0:384], lhsT=rb[:, :], rhs=oh[:, p * 384:(p + 1) * 384],
                                          start=True, stop=True), reads=[rbr, ohr], writes=[pgr])
            S.op("act", lambda e: e.activation(out=gt[:, p * 384:(p + 1) * 384], in_=pg[0:NH, 0:384], func=AF.Exp),
                 reads=[pgr], writes=[])
            S._put(gtr.w, (S.sem["act"], S.cnt["act"]))
        gdr = S.res("gtabd")
        S.dma("sp", [(gtab.ap(), gt[:])], reads=[gtr], parts=[gdr])
        anti = sb("anti", [128, 128], F32)
        antir = S.res("anti")
        S.dma("sp", [(anti[:], c_anti.ap())], writes=[antir])
        Xp = Pool_(S, ph, "X", [128, 768], F32, 2)
        Mp = Pool_(S, ph, "M", [128, 768], F32, 2)
        g2r = S.res("gtab2d")
        for h in range(NH):
            X, Xr = Xp.next()
            prs = []
            for p in range(3):
                for kt in range(2):
                    prs.append((X[:, (p * 2 + kt) * 128:(p * 2 + kt + 1) * 128],
                                dram_ap(gtab, h * 1152 + p * 384 + 129 - 128 * kt, [[1, 128], [1, 128]])))
            S.dma("sp", prs, reads=[gdr], writes=[Xr])
            M, Mr = Mp.next()
            for half in range(2):
                pg, pgr = pp.next()
                S.op("pe", lambda e: e.matmul(pg[:, 0:384], lhsT=anti[:], rhs=X[:, half * 384:(half + 1) * 384],
                                              start=True, stop=True), reads=[antir, Xr], writes=[pgr])
                S.op("dve", lambda e: e.tensor_copy(out=M[:, half * 384:(half + 1) * 384], in_=pg[:, 0:384]),
                     reads=[pgr], writes=([Mr] if half == 0 else []))
            S._put(Mr.w, (S.sem["dve"], S.cnt["dve"]))
            S.dma("sp", [(gtab2.ap()[h], M[:])], reads=[Mr], parts=[g2r])
        S.barrier()
        S.end_phase()


def phase_b(nc, S, C, l, qaT, qbT, kv_loc, kv_all, c_loc, c_all, c_sel, gtab2, yT, ccsem, cccnt):
    with ExitStack() as ph:
        S.begin_phase(ph)
        biasB = ph.enter_context(nc.sbuf_tensor(un("biasB"), [128, NH, 32, 16], F32))
        bBr = S.res("biasB")
        pairs = [[0, 1], [2, 3], [4, 5], [6, 7]]
        for i in range(8):
            nc.gpsimd.collective_compute("AllGather", ALU.bypass, replica_groups=pairs,
                                         ins=[kv_loc[i].ap().opt()], outs=[kv_all[i].ap().opt()]).then_inc(ccsem)
        nc.gpsimd.collective_compute("AllGather", ALU.bypass, replica_groups=pairs,
                                     ins=[c_loc.ap().opt()], outs=[c_all.ap().opt()]).then_inc(ccsem)
        cccnt[0] += 9
        for e in S.eng:
            S._wait(e, [(ccsem, cccnt[0])])
        with ExitStack() as tb:
            sb = lambda name, shape, dtype: tb.enter_context(nc.sbuf_tensor(un(name), shape, dtype))
            ncl = sb("ncl", [NH, T], F32)
            nclr = S.res("ncl")
            ncp = sb("ncp", [NH, T], F32)
            ncpr = S.res("ncp")
            sel = sb("sel", [NH, NH * 128], F32)
            selr = S.res("sel")
            cT = sb("cT", [128, 32, NH], F32)
            cTr = S.res("cT")
            ncref = sb("ncref", [128, NH, 16], F32)
            ncrr = S.res("ncref")
            S.dma("sp", [(ncl[:], c_loc.ap())], writes=[nclr])
            S.dma("sp", [(ncp[:], c_all.ap()[0:NH, :])], writes=[ncpr])
            S.dma("sp", [(sel[:], c_sel.ap())], writes=[selr])
            S.op("dve", lambda e: e.tensor_scalar(out=ncp[:], in0=ncp[:], scalar1=ncp[:, T - 1:T], scalar2=None,
                                                  op0=ALU.subtract), reads=[ncpr], writes=[ncpr])
            pb = Pool_(S, tb, "psb", [128, 512], F32, 2, psum=True)
            pc, pcr = pb.next()
            for blk in range(32):
                srct = ncp if blk < 16 else ncl
                srcr = ncpr if blk < 16 else nclr
                b16 = blk % 16
                S.op("pe", lambda e: e.transpose(out=pc[:, blk * NH:(blk + 1) * NH], in_=srct[:, b16 * 128:(b16 + 1) * 128],
                                                 identity=C.identf[0:NH, 0:NH]),
                     reads=[srcr, C.constr], writes=([pcr] if blk == 0 else []), inc=(blk == 31))
            S._put(pcr.w, (S.sem["pe"], S.cnt["pe"]))
            S.op("dve", lambda e: e.tensor_copy(out=cT[:].rearrange("p b h -> p (b h)"), in_=pc[:, 0:32 * NH]),
                 reads=[pcr], writes=[cTr])
            S.op("dve", lambda e: e.tensor_scalar(out=cT[:, 0:16, :], in0=cT[:, 0:16, :], scalar1=C.core[:, 0:1],
                                                  scalar2=None, op0=ALU.subtract), reads=[cTr, C.constr], writes=[cTr])
            pr_, prr = pb.next()
            for h in range(NH):
                S.op("pe", lambda e: e.matmul(pr_[:, h * 16:(h + 1) * 16], lhsT=sel[:, h * 128:(h + 1) * 128],
                                              rhs=ncl[:, 64:T:128], start=True, stop=True),
                     reads=[selr, nclr], writes=([prr] if h == 0 else []), inc=(h == NH - 1))
            S._put(prr.w, (S.sem["pe"], S.cnt["pe"]))
            S.op("dve", lambda e: e.tensor_copy(out=ncref[:].rearrange("p h q -> p (h q)"), in_=pr_[:, 0:NH * 16]),
                 reads=[prr], writes=[ncrr])
            for h in range(NH):
                for j in range(32):
                    S.op("dve", lambda e: e.tensor_scalar(out=biasB[:, h, j, :], in0=ncref[:, h, :],
                                                          scalar1=cT[:, j, h:h + 1], scalar2=-1.0,
                                                          op0=ALU.subtract, op1=ALU.mult),
                         reads=[ncrr, cTr], writes=[], inc=(j == 31))
                S._put(bBr.w, (S.sem["dve"], S.cnt["dve"]))
            S.barrier()
            S._put(bBr.w, (S.sem["dve"], S.cnt["dve"]))
        phase_b_main(nc, S, C, l, qaT, qbT, kv_loc, kv_all, gtab2, yT, biasB, bBr, ph)
        S.end_phase()


def phase_b_main(nc, S, C, l, qaT, qbT, kv_loc, kv_all, gtab2, yT, biasB, bBr, ph):
    kaT_o = 0
    kbT_o = NH * E * T
    va_o = 2 * NH * E * T
    vb_o = 3 * NH * E * T
    sb = lambda name, shape, dtype: ph.enter_context(nc.sbuf_tensor(un(name), shape, dtype))
    psS = Pool_(S, ph, "psS", [128, 512], F32, 2, psum=True)
    psY = Pool_(S, ph, "psY", [128, 512], F32, 1, psum=True)
    psL = Pool_(S, ph, "psL", [128, 512], F32, 1, psum=True)
    psD = Pool_(S, ph, "psD", [128, 1024], F32, 1, psum=True)
    psU = Pool_(S, ph, "psU", [128, 512], F32, 1, psum=True)
    psZ = Pool_(S, ph, "psZ", [128, 512], F32, 1, psum=True)
    kTp = Pool_(S, ph, "kT", [128, 2 * T], BF16, 2)
    qTp = Pool_(S, ph, "qT", [128, T], BF16, 2)
    kTdp = Pool_(S, ph, "kTd", [128, 2 * T], BF16, 2)
    qTdp = Pool_(S, ph, "qTd", [128, T], BF16, 2)
    Vp = Pool_(S, ph, "Vf", [128, 32, 128], BF16, 2)
    PTp = Pool_(S, ph, "PT", [128, 512], BF16, 4)
    sYp = Pool_(S, ph, "sY", [128, 512], F32, 2)
    sLp = Pool_(S, ph, "sL", [128, 512], F32, 2)
    yop = Pool_(S, ph, "yo", [128, 512], BF16, 2)
    Vd = [Pool_(S, ph, f"Vd{p}", [128, DIL[p] + 16, 128], BF16, 1) for p in range(3)]
    Ecp = Pool_(S, ph, "Ec", [128, 3, 2, 2, 128], F32, 1)
    Pexp = Pool_(S, ph, "Pexp", [128, 1024], F32, 2)
    PTd = Pool_(S, ph, "PTd", [128, 1024], BF16, 2)
    accY = sb("accY", [128, T], F32)
    accL = sb("accL", [128, T], F32)
    accr = S.res("acc")
    yd = sb("yd", [128, T], BF16)
    ydr = S.res("yd")
    yTr = S.res("yTd")

    def kv_src(base, off, pattern, slot0):
        return kv_ap(kv_all if slot0 else kv_loc, base + off, pattern)

    def fox_stream():
        for h in range(NH):
            kT, kTr = kTp.next()
            qT, qTr = qTp.next()
            V, Vr = Vp.next()
            S.dma("sp", [(kT[:, 0:T], kv_src(kaT_o, h * E * T, [[T, 128], [1, T]], True)),
                         (kT[:, T:2 * T], kv_src(kaT_o, h * E * T, [[T, 128], [1, T]], False))], writes=[kTr])
            S.dma("sp", [(qT[:], qaT.ap()[h])], writes=[qTr])
            prs = []
            for s0_, b0 in ((True, 0), (False, 16)):
                for hf in range(2):
                    prs.append((V[:, b0 + 8 * hf:b0 + 8 * hf + 8, :],
                                kv_src(va_o, hf * CH + h * E, [[1024, 128], [128 * 1024, 8], [1, 128]], s0_)))
            S.dma("sp", prs, writes=[Vr])
            tiles = []
            for G in range(4):
                nkb = 16 + 4 * G + 4
                for j in range(nkb):
                    tiles.append((G, j, nkb))

            def qk(idx):
                G, j, nkb = tiles[idx]
                c0 = 128 * max(j - (16 + 4 * G), 0)
                pS, pSr = psS.next()
                S.op("pe", lambda e: e.matmul(pS[:, c0:512], lhsT=kT[:, j * 128:(j + 1) * 128],
                                              rhs=qT[:, G * 512 + c0:(G + 1) * 512], start=True, stop=True),
                     reads=[kTr, qTr], writes=[pSr])
                return pS, pSr

            cur = {}
            pend = qk(0)
            for idx, (G, j, nkb) in enumerate(tiles):
                o = j - (16 + 4 * G)
                c0 = 128 * max(o, 0)
                pS, pSr = pend
                if j == 0:
                    cur["Y"] = psY.next()
                    cur["L"] = psL.next()
                pY, pYr = cur["Y"]
                pL, pLr = cur["L"]
                PT, PTr = PTp.next()
                nq = 0
                for qb in range(max(o, 0), 4):
                    S.op("act", lambda e: e.activation(out=PT[:, qb * 128:(qb + 1) * 128], in_=pS[:, qb * 128:(qb + 1) * 128],
                                                       func=AF.Exp, scale=SCALE, bias=biasB[:, h, j, 4 * G + qb:4 * G + qb + 1]),
                         reads=[pSr, bBr], writes=([PTr] if nq == 0 else []), inc=True)
                    nq += 1
                S._put(PTr.w, (S.sem["act"], S.cnt["act"]))
                if idx + 1 < len(tiles):
                    pend = qk(idx + 1)
                if o >= 0:
                    S.op("dve", lambda e: e.tensor_tensor(out=PT[:, c0:c0 + 128], in0=PT[:, c0:c0 + 128], in1=C.tri[:],
                                                          op=ALU.mult), reads=[PTr, C.constr], writes=[PTr])
                first = (j == 0)
                last = (j == nkb - 1)
                S.op("pe", lambda e: e.matmul(pY[:, c0:512], lhsT=V[:, j, :], rhs=PT[:, c0:512], start=first, stop=last),
                     reads=[Vr, PTr], writes=([pYr] if (first or last) else []), inc=False)
                S.op("pe", lambda e: e.matmul(pL[:, c0:512], lhsT=C.ones[:], rhs=PT[:, c0:512], start=first, stop=last),
                     reads=[PTr, C.constr], writes=([pLr] if (first or last) else []), inc=True)
                if last:
                    sY, sYr = sYp.next()
                    sL, sLr = sLp.next()
                    S.op("dve", lambda e: e.tensor_copy(out=sY[:], in_=pY[:]), reads=[pYr], writes=[sYr])
                    S.op("dve", lambda e: e.tensor_copy(out=sL[:], in_=pL[:]), reads=[pLr], writes=[sLr])
                    S.op("dve", lambda e: e.reciprocal(out=sL[:], in_=sL[:]), reads=[sLr], writes=[sLr])
                    yo, yor = yop.next()
                    S.op("dve", lambda e: e.tensor_tensor(out=yo[:], in0=sY[:], in1=sL[:], op=ALU.mult),
                         reads=[sYr, sLr], writes=[yor])
                    S.dma("sp", [(yT.ap()[h, :, G * 512:(G + 1) * 512], yo[:])], reads=[yor], parts=[yTr])
                yield

    def dil_stream():
        for h in range(NH):
            kT, kTr = kTdp.next()
            qT, qTr = qTdp.next()
            S.dma("sp", [(kT[:, 0:T], kv_src(kbT_o, h * E * T, [[T, 128], [1, T]], True)),
                         (kT[:, T:2 * T], kv_src(kbT_o, h * E * T, [[T, 128], [1, T]], False))], writes=[kTr])
            S.dma("sp", [(qT[:], qbT.ap()[h])], writes=[qTr])
            Ec, Ecr = Ecp.next()
            for p_ in range(3):
                S.dma("sp", [(Ec[:, p_, 0, :, :], gtab2.ap()[h, :, p_ * 256:(p_ + 1) * 256].rearrange("j (k i) -> j k i", k=2))],
                      writes=([Ecr] if p_ == 0 else []), parts=([] if p_ == 0 else [Ecr]))
            S.op("dve", lambda e: e.tensor_scalar(out=Ec[:, :, 1, 0, :], in0=Ec[:, :, 0, 0, :], scalar1=C.core[:, 1:2],
                                                  scalar2=None, op0=ALU.mult), reads=[Ecr, C.constr], writes=[], inc=True)
            S.op("dve", lambda e: e.tensor_copy(out=Ec[:, :, 1, 1, :], in_=Ec[:, :, 0, 1, :]), reads=[Ecr], writes=[], inc=True)
            S._put(Ecr.w, (S.sem["dve"], S.cnt["dve"]))
            vres = []
            for p, d in enumerate(DIL):
                Vt, Vtr = Vd[p].next()
                nbn = T // (128 * d)
                prs = []
                if d == 1:
                    for hf in range(2):
                        prs.append((Vt[:, 1 + 8 * hf:9 + 8 * hf, :],
                                    kv_src(vb_o, hf * CH + h * E, [[1024, 128], [128 * 1024, 8], [1, 128]], False)))
                    prs.append((Vt[:, 0:1, :], kv_src(vb_o, (T - 128) * 1024 + h * E, [[1024, 128], [1024, 1], [1, 128]], True)))
                elif d == 4:
                    for nb_ in range(nbn):
                        prs.append((Vt[:, d + nb_ * d:d + (nb_ + 1) * d, :],
                                    kv_src(vb_o, nb_ * 128 * d * 1024 + h * E, [[d * 1024, 128], [1024, d], [1, 128]], False)))
                    prs.append((Vt[:, 0:d, :], kv_src(vb_o, (T - 128 * d) * 1024 + h * E, [[d * 1024, 128], [1024, d], [1, 128]], True)))
                else:
                    for hf in range(2):
                        prs.append((Vt[64 * hf:64 * hf + 64, 16:32, :],
                                    kv_src(vb_o, hf * CH + h * E, [[16 * 1024, 64], [1024, 16], [1, 128]], False)))
                        prs.append((Vt[64 * hf:64 * hf + 64, 0:16, :],
                                    kv_src(vb_o, hf * CH + h * E, [[16 * 1024, 64], [1024, 16], [1, 128]], True)))
                S.dma("sp", prs, writes=[Vtr])
                vres.append((Vt, Vtr))
            yield
            claim(S, "dve", accr)
            accr.w = {}
            accr.r = {}
            batches = []
            for p, d in enumerate(DIL):
                for bt in range(4):
                    units = []
                    for ub in range(4):
                        if d == 1:
                            units.append((4 * bt + ub, 0))
                        elif d == 4:
                            units.append((bt, ub))
                        else:
                            units.append((0, 4 * bt + ub))
                    batches.append((p, d, bt, units))

            def qk(bi):
                p, d, bt, units = batches[bi]
                pD, pDr = psD.next()
                for ub, (nb, r) in enumerate(units):
                    q0 = nb * 128 * d + r
                    qs = qT[:, q0:q0 + 127 * d + 1:d]
                    kc_ = kT[:, T + q0:T + q0 + 127 * d + 1:d]
                    kp_ = kT[:, T + q0 - 128 * d:T + q0 - d + 1:d]
                    S.op("pe", lambda e: e.matmul(pD[:, ub * 256:ub * 256 + 128], lhsT=kp_, rhs=qs, start=True, stop=True),
                         reads=[kTr, qTr], writes=([pDr] if ub == 0 else []), inc=False)
                    S.op("pe", lambda e: e.matmul(pD[:, ub * 256 + 128:ub * 256 + 256], lhsT=kc_, rhs=qs, start=True, stop=True),
                         reads=[kTr, qTr], writes=[], inc=(ub == 3))
                S._put(pDr.w, (S.sem["pe"], S.cnt["pe"]))
                return pD, pDr

            pend = qk(0)
            yield
            for bi, (p, d, bt, units) in enumerate(batches):
                Vt, Vtr = vres[p]
                pD, pDr = pend
                Pe, Per = Pexp.next()
                S.op("act", lambda e: e.activation(out=Pe[:, 0:512], in_=pD[:, 0:512], func=AF.Exp, scale=SCALE),
                     reads=[pDr], writes=[Per])
                S.op("act", lambda e: e.activation(out=Pe[:, 512:1024], in_=pD[:, 512:1024], func=AF.Exp, scale=SCALE),
                     reads=[pDr], writes=[])
                S._put(Per.w, (S.sem["act"], S.cnt["act"]))
                yield
                if bi + 1 < len(batches):
                    pend = qk(bi + 1)
                Pt, Ptr = PTd.next()
                claim(S, "dve", Ptr)
                Ptr.w = {}
                Ptr.r = {}
                for ub, (nb, r) in enumerate(units):
                    var = 1 if nb == 0 else 0
                    S.op("dve", lambda e: e.tensor_tensor(out=Pt[:, ub * 256:(ub + 1) * 256], in0=Pe[:, ub * 256:(ub + 1) * 256],
                                                          in1=Ec[:, p, var, :, :].rearrange("p k i -> p (k i)"), op=ALU.mult),
                         reads=[Per, Ecr], writes=[], inc=True)
                S._put(Ptr.w, (S.sem["dve"], S.cnt["dve"]))
                yield
                yield
                pU, pUr = psU.next()
                pZ, pZr = psZ.next()
                for ub, (nb, r) in enumerate(units):
                    sp_ = nb * d + r
                    sc_ = (nb + 1) * d + r
                    f_ = (ub == 0)
                    S.op("pe", lambda e: e.matmul(pU[:, ub * 128:(ub + 1) * 128], lhsT=Vt[:, sp_, :],
                                                  rhs=Pt[:, ub * 256:ub * 256 + 128], start=True, stop=False),
                         reads=[Vtr, Ptr], writes=([pUr] if f_ else []), inc=False)
                    S.op("pe", lambda e: e.matmul(pU[:, ub * 128:(ub + 1) * 128], lhsT=Vt[:, sc_, :],
                                                  rhs=Pt[:, ub * 256 + 128:ub * 256 + 256], start=False, stop=True),
                         reads=[Vtr, Ptr], writes=[], inc=False)
                    S.op("pe", lambda e: e.matmul(pZ[:, ub * 128:(ub + 1) * 128], lhsT=C.ones[:],
                                                  rhs=Pt[:, ub * 256:ub * 256 + 128], start=True, stop=False),
                         reads=[C.constr, Ptr], writes=([pZr] if f_ else []), inc=False)
                    S.op("pe", lambda e: e.matmul(pZ[:, ub * 128:(ub + 1) * 128], lhsT=C.ones[:],
                                                  rhs=Pt[:, ub * 256 + 128:ub * 256 + 256], start=False, stop=True),
                         reads=[C.constr, Ptr], writes=[], inc=(ub == 3))
                S._put(pUr.w, (S.sem["pe"], S.cnt["pe"]))
                S._put(pZr.w, (S.sem["pe"], S.cnt["pe"]))
                yield
                if d == 1:
                    oy = accY[:, bt * 512:(bt + 1) * 512]
                    ol = accL[:, bt * 512:(bt + 1) * 512]
                    iu = pU[:]
                    iz = pZ[:]
                elif d == 4:
                    oy = accY[:, bt * 512:(bt + 1) * 512].rearrange("p (i r) -> p r i", r=4)
                    ol = accL[:, bt * 512:(bt + 1) * 512].rearrange("p (i r) -> p r i", r=4)
                    iu = pU[:].rearrange("p (r i) -> p r i", r=4)
                    iz = pZ[:].rearrange("p (r i) -> p r i", r=4)
                else:
                    oy = accY[:].rearrange("p (i r) -> p r i", r=16)[:, 4 * bt:4 * bt + 4, :]
                    ol = accL[:].rearrange("p (i r) -> p r i", r=16)[:, 4 * bt:4 * bt + 4, :]
                    iu = pU[:].rearrange("p (r i) -> p r i", r=4)
                    iz = pZ[:].rearrange("p (r i) -> p r i", r=4)
                if p == 0:
                    S.op("dve", lambda e: e.tensor_copy(out=oy, in_=iu), reads=[pUr], writes=[], inc=True)
                    S.op("dve", lambda e: e.tensor_copy(out=ol, in_=iz), reads=[pZr], writes=[], inc=True)
                else:
                    S.op("dve", lambda e: e.tensor_tensor(out=oy, in0=iu, in1=oy, op=ALU.add), reads=[pUr, accr], writes=[], inc=True)
                    S.op("dve", lambda e: e.tensor_tensor(out=ol, in0=iz, in1=ol, op=ALU.add), reads=[pZr, accr], writes=[], inc=True)
                S._put(accr.w, (S.sem["dve"], S.cnt["dve"]))
                yield
            S.op("dve", lambda e: e.reciprocal(out=accL[:], in_=accL[:]), reads=[accr], writes=[accr])
            S.op("dve", lambda e: e.tensor_tensor(out=yd[:], in0=accY[:], in1=accL[:], op=ALU.mult), reads=[accr], writes=[ydr])
            S.dma("sp", [(yT.ap()[NH + h], yd[:])], reads=[ydr], parts=[yTr])
            yield

    fs = fox_stream()
    ds = dil_stream()
    f_alive = d_alive = True
    while f_alive or d_alive:
        if f_alive:
            try:
                next(fs)
            except StopIteration:
                f_alive = False
        if d_alive:
            try:
                next(ds)
            except StopIteration:
                d_alive = False
    S.barrier()


def phase_c(nc, S, C, l, xsrc, xdst, x1, yT, on_a, on_b, w_out, norm2_g, w1, w2):
    with ExitStack() as ph:
        S.begin_phase(ph)
        sb = lambda name, shape, dtype: ph.enter_context(nc.sbuf_tensor(un(name), shape, dtype))
        C.stat = Pool_(S, ph, "stat", [128, 4], F32, 4)
        C.hb = Pool_(S, ph, "hb", [128, D], BF16, 2)
        C.junk = C.hb
        C.pst = Pool_(S, ph, "pst", [128, 512], F32, 2, psum=True)
        pss = Pool_(S, ph, "pss", [128, 512], F32, 2, psum=True)
        psm = Pool_(S, ph, "psm", [128, 512], F32, 4, psum=True)
        xrow = Pool_(S, ph, "xrow", [128, D], F32, 2)
        xpc = Pool_(S, ph, "xpc", [128, 512], F32, 4)
        abp = Pool_(S, ph, "ab", [128, 16, 512], BF16, 2)
        wpool = Pool_(S, ph, "wt", [128, 16, 512], BF16, 3)
        sqp = Pool_(S, ph, "sq", [128, 512], BF16, 2)
        rlp = Pool_(S, ph, "rl", [128, 512], F32, 2)
        h1T = sb("h1T", [128, 64, 512], BF16)
        h1r = S.res("h1T")
        rs = sb("rs", [128, 2, 512], F32)
        rsg = [S.res("rs0"), S.res("rs1")]
        gbc = sb("gbc", [128, D], F32)
        gr = S.res("gbc")
        gcol = sb("gcol", [128, 16], F32)
        gcr = S.res("gcol")
        S.dma("sp", [(gbc[:], dram_ap(norm2_g, l * D, [[0, 128], [1, D]]))], writes=[gr])
        S.dma("sp", [(gcol[:, 0:8], dram_ap(on_a, l * 1024, [[1, 128], [128, 8]])),
                     (gcol[:, 8:16], dram_ap(on_b, l * 1024, [[1, 128], [128, 8]]))], writes=[gcr],
              allow_slow_non_contiguous=True)
        wo = w_out.ap()[l]
        w1l = w1.ap()[l]
        w2l = w2.ap()[l]
        x1r = S.res("x1d")
        xdr = S.res("xdst")

        for rt in range(4):
            t0 = rt * 512
            A, Ar = abp.next()
            S.dma("sp", [(A[:], yT.ap()[:, :, t0:t0 + 512].rearrange("h e t -> e h t"))], writes=[Ar])
            for grp in range(2):
                pq, pqr = pss.next()
                for hh in range(8):
                    sq, sqr = sqp.next()
                    S.op("dve", lambda e: e.tensor_tensor(out=sq[:], in0=A[:, grp * 8 + hh, :], in1=A[:, grp * 8 + hh, :],
                                                          op=ALU.mult), reads=[Ar], writes=[sqr])
                    S.op("pe", lambda e: e.matmul(pq[:], lhsT=C.ones[:], rhs=sq[:], start=(hh == 0), stop=(hh == 7)),
                         reads=[sqr, C.constr], writes=([pqr] if (hh == 0 or hh == 7) else []), inc=True)
                S.op("act", lambda e: e.activation(out=rs[:, grp, :], in_=pq[:], func=AF.Ln, scale=1.0 / 1024,
                                                   bias=C.eps_t[:, 0:1]), reads=[pqr, C.constr], writes=[rsg[grp]])
                S.op("act", lambda e: e.activation(out=rs[:, grp, :], in_=rs[:, grp, :], func=AF.Exp, scale=-0.5),
                     reads=[rsg[grp]], writes=[rsg[grp]])
            for hh in range(16):
                grp = hh // 8
                S.op("dve", lambda e: e.scalar_tensor_tensor(out=A[:, hh, :], in0=A[:, hh, :], scalar=gcol[:, hh:hh + 1],
                                                             in1=rs[:, grp, :], op0=ALU.mult, op1=ALU.mult),
                     reads=[rsg[grp], gcr, Ar], writes=[])
            S._put(Ar.w, (S.sem["dve"], S.cnt["dve"]))
            for cg in range(4):
                wt, wr = load_wtile(S, wpool, wo, 0, cg * 512)
                for st in range(4):
                    xp, xpr = xpc.next()
                    S.dma("sp", [(xp[:], xsrc.ap()[t0 + st * 128:t0 + (st + 1) * 128, cg * 512:(cg + 1) * 512])],
                          writes=[xpr])
                    pm, pmr = psm.next()
                    for kc in range(16):
                        S.op("pe", lambda e, kc=kc: e.matmul(pm[:], lhsT=A[:, kc, st * 128:(st + 1) * 128],
                                                             rhs=wt[:, kc, :], start=(kc == 0), stop=(kc == 15)),
                             reads=[wr, Ar], writes=([pmr] if kc == 0 else []), inc=(kc == 15))
                    S.op("dve", lambda e: e.tensor_tensor(out=xp[:], in0=pm[:], in1=xp[:], op=ALU.add),
                         reads=[pmr], writes=[xpr])
                    S.dma("sp", [(x1.ap()[t0 + st * 128:t0 + (st + 1) * 128, cg * 512:(cg + 1) * 512], xp[:])],
                          reads=[xpr], parts=[x1r])
            B, Br = abp.next()
            claim(S, "act", Br)
            claim(S, "dve", Br)
            Br.w = {}
            Br.r = {}
            for st in range(4):
                xt, xr = xrow.next()
                S.dma("sp", [(xt[:], x1.ap()[t0 + st * 128:t0 + (st + 1) * 128, :])], reads=[x1r], writes=[xr])
                rmsnorm_to_hT(S, C, xt, xr, gbc, gr, B, Br, st)
            claim(S, "dve", h1r)
            h1r.w = {}
            h1r.r = {}
            for ft in range(16):
                wt, wr = load_wtile(S, wpool, w1l, 0, ft * 512)
                for fb in range(4):
                    pm, pmr = psm.next()
                    for kc in range(16):
                        S.op("pe", lambda e, kc=kc: e.matmul(pm[:], lhsT=wt[:, kc, fb * 128:(fb + 1) * 128],
                                                             rhs=B[:, kc, :], start=(kc == 0), stop=(kc == 15)),
                             reads=[wr, Br], writes=([pmr] if kc == 0 else []), inc=(kc == 15))
                    rl, rlr = rlp.next()
                    S.op("act", lambda e: e.activation(out=rl[:], in_=pm[:], func=AF.Relu), reads=[pmr], writes=[rlr])
                    S.op("dve", lambda e: e.tensor_tensor(out=h1T[:, ft * 4 + fb, :], in0=rl[:], in1=rl[:], op=ALU.mult),
                         reads=[rlr], writes=[])
                    S._put(h1r.w, (S.sem["dve"], S.cnt["dve"]))
            for cg in range(4):
                banks = [psm.next() for _ in range(4)]
                for q in range(4):
                    wt, wr = load_wtile(S, wpool, w2l, q * 2048, cg * 512)
                    for st in range(4):
                        pm, pmr = banks[st]
                        for kc in range(16):
                            first = (q == 0 and kc == 0)
                            last = (q == 3 and kc == 15)
                            S.op("pe", lambda e, kc=kc: e.matmul(pm[:], lhsT=h1T[:, q * 16 + kc, st * 128:(st + 1) * 128],
                                                                 rhs=wt[:, kc, :], start=first, stop=last),
                                 reads=[wr, h1r], writes=([pmr] if (first or last) else []), inc=(kc == 15))
                for st in range(4):
                    pm, pmr = banks[st]
                    xp, xpr = xpc.next()
                    S.dma("sp", [(xp[:], x1.ap()[t0 + st * 128:t0 + (st + 1) * 128, cg * 512:(cg + 1) * 512])],
                          reads=[x1r], writes=[xpr])
                    S.op("dve", lambda e: e.tensor_tensor(out=xp[:], in0=pm[:], in1=xp[:], op=ALU.add),
                         reads=[pmr], writes=[xpr])
                    S.dma("sp", [(xdst.ap()[t0 + st * 128:t0 + (st + 1) * 128, cg * 512:(cg + 1) * 512], xp[:])],
                          reads=[xpr], parts=[xdr])
        S.barrier()
        S.end_phase()


def phase_final(nc, S, C, x2, fin_g, y_out):
    with ExitStack() as ph:
        S.begin_phase(ph)
        sb = lambda name, shape, dtype: ph.enter_context(nc.sbuf_tensor(un(name), shape, dtype))
        C.stat = Pool_(S, ph, "stat", [128, 4], F32, 4)
        C.junk = Pool_(S, ph, "junk", [128, D], BF16, 2)
        xrow = Pool_(S, ph, "xrow", [128, D], F32, 3)
        gbc = sb("gbc", [128, D], F32)
        gr = S.res("gbc")
        yr = S.res("yout")
        S.dma("sp", [(gbc[:], dram_ap(fin_g, 0, [[0, 128], [1, D]]))], writes=[gr])
        for i in range(16):
            xt, xr = xrow.next()
            S.dma("sp", [(xt[:], x2.ap()[i * 128:(i + 1) * 128, :])], writes=[xr])
            rmsnorm_to_hT(S, C, xt, xr, gbc, gr, None, None, 0, out_norm=(xt, xr))
            S.dma("sp", [(y_out.ap()[i * 128:(i + 1) * 128, :], xt[:])], reads=[xr], parts=[yr])
        S.barrier()
        S.end_phase()

def host_consts():
    ident = np.eye(128, dtype=np.float32)
    s = np.arange(128)
    tri = (s[:, None] <= s[None, :]).astype(np.float32)
    sel = np.zeros((8, 8, 128), np.float32)
    for h in range(8):
        sel[h, h, :] = 1.0
    oh = np.zeros((33, 3, 384), np.float32)
    for p, d in enumerate(DIL):
        for xx in range(384):
            rel = xx - 128
            if rel < 0 or rel > 128:
                oh[32, p, xx] = 1.0
                continue
            dist = rel * d
            if dist < 16:
                b = dist
            else:
                dd = np.float32(max(dist, 1))
                v = np.log(dd / np.float32(16)) / np.float32(np.log(2048 / 16)) * np.float32(16)
                b = min(16 + int(np.float32(v)), 31)
            oh[b, p, xx] = 1.0
    return {
        "c_ident": ident.astype(ml_dtypes.bfloat16),
        "c_tri": tri.astype(ml_dtypes.bfloat16),
        "c_identf": ident,
        "c_anti": np.ascontiguousarray(ident[::-1]),
        "c_sel": sel.reshape(8, 8 * 128),
        "c_oh": oh.reshape(33, 3 * 384),
    }


_NC_CACHE = {}


def make_in_maps(inputs):
    consts = host_consts()
    x = np.asarray(inputs["x"], dtype=np.float32)
    shared = {k: np.ascontiguousarray(np.asarray(v, dtype=np.float32)) for k, v in inputs.items() if k != "x"}
    in_maps = []
    for c in range(NCORES):
        b, half = c // 2, c % 2
        m = dict(shared)
        m.update(consts)
        m["x"] = np.ascontiguousarray(x[b, half * T:(half + 1) * T, :])
        core = np.zeros((128, 2), np.float32)
        core[:, 0] = 30000.0 if half == 0 else 0.0
        core[:, 1] = 0.0 if half == 0 else 1.0
        m["c_core"] = core
        in_maps.append(m)
    return in_maps


def kernel(**inputs):
    if "nc" not in _NC_CACHE:
        _NC_CACHE["nc"] = build()
    nc = _NC_CACHE["nc"]
    in_maps = make_in_maps(inputs)
    res = run_bass_kernel_spmd(nc, in_maps, core_ids=list(range(NCORES)))
    out = np.zeros((4, SEQ, D), np.float32)
    for c in range(NCORES):
        b, half = c // 2, c % 2
        out[b, half * T:(half + 1) * T, :] = np.asarray(res.results[c]["y"], dtype=np.float32)
    return out
```
